# Optimizing a Trainium2 kernel written in Bass

```python
import math
import jax, jax.numpy as jnp
from jax import lax
import numpy as np

D_MODEL = 1024
BATCH = 8
SEQ = 4096
DEPTH = 2

D_FF = 2816
HEAD_DIM = 64
ROPE_DIM = HEAD_DIM // 4
ROPE_THETA = 500000.0
EPS = 1e-6
Q_BLOCK = 128
DIFF_HEADS = 4
DIFF_V_HEAD = 2 * HEAD_DIM
NSA_HEADS = 8
NSA_GROUPS = 2
NSA_HPG = NSA_HEADS // NSA_GROUPS
CMP_BLOCK = 32
CMP_STRIDE = 16
CMP_HIDDEN = 256
SEL_BLOCK = 64
SEL_TOPK = 8
WINDOW = 512
FORCED_SCORE = 1.0e4
DIFF_QK = DIFF_HEADS * 2 * HEAD_DIM
DIFF_VW = DIFF_HEADS * DIFF_V_HEAD
NSA_QW = NSA_HEADS * HEAD_DIM
NSA_KVW = NSA_GROUPS * HEAD_DIM
NSA_GATEW = 3 * NSA_HEADS
SPLIT_SIZES = (DIFF_QK, DIFF_QK, DIFF_VW, NSA_QW, NSA_KVW, NSA_KVW, NSA_KVW, NSA_KVW, NSA_KVW, NSA_KVW, NSA_GATEW, D_MODEL, D_MODEL)
IN_COLS = sum(SPLIT_SIZES)

kernel_name = "hybrid_diffattn_nsa_macaron"


def rms_norm(x, g=None):
    xf = x.astype(jnp.float32)
    y = xf * lax.rsqrt(jnp.mean(xf * xf, axis=-1, keepdims=True) + EPS)
    if g is not None:
        y = y * g.astype(jnp.float32)
    return y.astype(x.dtype)


def swiglu(x, w_gu, w_down):
    g, u = jnp.split(x @ w_gu, 2, axis=-1)
    return (jax.nn.silu(g) * u) @ w_down


def rope_partial(x, positions):
    half = ROPE_DIM // 2
    inv = ROPE_THETA ** (-2.0 * jnp.arange(half, dtype=jnp.float32) / ROPE_DIM)
    ang = positions.astype(jnp.float32)[..., None] * inv
    cos = jnp.cos(ang)[:, :, None, :]
    sin = jnp.sin(ang)[:, :, None, :]
    xr = x[..., :ROPE_DIM].astype(jnp.float32)
    x1, x2 = xr[..., :half], xr[..., half:]
    rot = jnp.concatenate([x1 * cos - x2 * sin, x2 * cos + x1 * sin], axis=-1).astype(x.dtype)
    return jnp.concatenate([rot, x[..., ROPE_DIM:]], axis=-1)


def masked_softmax(s, mask):
    s = jnp.where(mask, s.astype(jnp.float32), -jnp.inf)
    m = jnp.max(s, axis=-1, keepdims=True)
    m = jnp.where(jnp.isfinite(m), m, 0.0)
    p = jnp.exp(s - m)
    return p / jnp.maximum(jnp.sum(p, axis=-1, keepdims=True), 1e-30)


def diff_attention(q, k, v, lam, lam_init):
    B, T = q.shape[:2]
    nqb = T // Q_BLOCK
    scale = HEAD_DIM ** -0.5
    qb = q.reshape(B, nqb, Q_BLOCK, DIFF_HEADS, 2, HEAD_DIM).transpose(1, 0, 2, 3, 4, 5)
    kpos = jnp.arange(T)

    def block(args):
        qi, i = args
        qpos = i * Q_BLOCK + jnp.arange(Q_BLOCK)
        s = jnp.einsum('bqhmd,bkhmd->bhmqk', qi, k) * scale
        p = masked_softmax(s, kpos[None, :] <= qpos[:, None])
        a = p[:, :, 0] - lam * p[:, :, 1]
        return jnp.einsum('bhqk,bkhe->bqhe', a.astype(v.dtype), v)

    o = lax.map(block, (qb, jnp.arange(nqb)))
    o = o.transpose(1, 0, 2, 3, 4).reshape(B, T, DIFF_HEADS, DIFF_V_HEAD)
    o = rms_norm(o) * (1.0 - lam_init)
    return o.reshape(B, T, DIFF_VW)


def compress(x, pos_emb, w1, w2):
    B, T, G, dh = x.shape
    n_cmp = (T - CMP_BLOCK) // CMP_STRIDE + 1
    idx = jnp.arange(n_cmp)[:, None] * CMP_STRIDE + jnp.arange(CMP_BLOCK)[None, :]
    blocks = x[:, idx] + pos_emb[None, None, :, None, :]
    flat = blocks.transpose(0, 1, 3, 2, 4).reshape(B, n_cmp, G, CMP_BLOCK * dh)
    return jax.nn.silu(flat @ w1) @ w2


def nsa_attention(q, kc, vc, ks, vs, kw, vw, gates):
    B, T = q.shape[:2]
    G, Hg, dh = NSA_GROUPS, NSA_HPG, HEAD_DIM
    nqb = T // Q_BLOCK
    scale = dh ** -0.5
    n_cmp = kc.shape[1]
    n_sel = T // SEL_BLOCK
    topk = min(SEL_TOPK, n_sel)
    cmp_start = jnp.arange(n_cmp) * CMP_STRIDE
    cmp_end = cmp_start + CMP_BLOCK - 1
    sel_start = jnp.arange(n_sel) * SEL_BLOCK
    ov = jnp.clip(jnp.minimum(cmp_start[:, None] + CMP_BLOCK, sel_start[None, :] + SEL_BLOCK)
                  - jnp.maximum(cmp_start[:, None], sel_start[None, :]), 0)
    overlap = ov.astype(jnp.float32) / CMP_BLOCK
    ksb = ks.reshape(B, n_sel, SEL_BLOCK, G, dh).transpose(0, 3, 1, 2, 4)
    vsb = vs.reshape(B, n_sel, SEL_BLOCK, G, dh).transpose(0, 3, 1, 2, 4)
    pad = ((0, 0), (WINDOW, 0), (0, 0), (0, 0))
    kw_pad = jnp.pad(kw, pad)
    vw_pad = jnp.pad(vw, pad)
    qb = q.reshape(B, nqb, Q_BLOCK, G, Hg, dh).transpose(1, 0, 2, 3, 4, 5)
    gb = gates.reshape(B, nqb, Q_BLOCK, G, Hg, 3).transpose(1, 0, 2, 3, 4, 5)
    bidx = jnp.arange(B)[:, None, None, None]
    gidx = jnp.arange(G)[None, None, :, None]
    jsel = jnp.arange(n_sel)

    def block(args):
        qi, gi, i = args
        qpos = i * Q_BLOCK + jnp.arange(Q_BLOCK)
        s_c = jnp.einsum('bqghd,bcgd->bqghc', qi, kc) * scale
        m_c = (cmp_end[None, :] <= qpos[:, None])[None, :, None, None, :]
        p_c = masked_softmax(s_c, m_c)
        o_c = jnp.einsum('bqghc,bcgd->bqghd', p_c.astype(vc.dtype), vc)
        imp = jnp.einsum('bqghc,cj->bqgj', p_c, overlap)
        cur = qpos // SEL_BLOCK
        forced = (jsel[None, :] == 0) | (jsel[None, :] == cur[:, None]) | (jsel[None, :] == cur[:, None] - 1)
        future = sel_start[None, :] > qpos[:, None]
        imp = jnp.where(future[None, :, None, :], -1.0, jnp.where(forced[None, :, None, :], FORCED_SCORE, imp))
        _, sel = lax.top_k(imp, topk)
        kg = ksb[bidx, gidx, sel].reshape(B, Q_BLOCK, G, topk * SEL_BLOCK, dh)
        vg = vsb[bidx, gidx, sel].reshape(B, Q_BLOCK, G, topk * SEL_BLOCK, dh)
        tok = (sel[..., None] * SEL_BLOCK + jnp.arange(SEL_BLOCK)).reshape(B, Q_BLOCK, G, topk * SEL_BLOCK)
        m_s = (tok <= qpos[None, :, None, None])[:, :, :, None, :]
        s_s = jnp.einsum('bqghd,bqgkd->bqghk', qi, kg) * scale
        p_s = masked_softmax(s_s, m_s)
        o_s = jnp.einsum('bqghk,bqgkd->bqghd', p_s.astype(vg.dtype), vg)
        kwi = lax.dynamic_slice_in_dim(kw_pad, i * Q_BLOCK, WINDOW + Q_BLOCK, axis=1)
        vwi = lax.dynamic_slice_in_dim(vw_pad, i * Q_BLOCK, WINDOW + Q_BLOCK, axis=1)
        kpos = i * Q_BLOCK - WINDOW + jnp.arange(WINDOW + Q_BLOCK)
        m_w = ((kpos[None, :] <= qpos[:, None]) & (kpos[None, :] > qpos[:, None] - WINDOW)
               & (kpos[None, :] >= 0))[None, :, None, None, :]
        s_w = jnp.einsum('bqghd,bkgd->bqghk', qi, kwi) * scale
        p_w = masked_softmax(s_w, m_w)
        o_w = jnp.einsum('bqghk,bkgd->bqghd', p_w.astype(vwi.dtype), vwi)
        return gi[..., 0:1] * o_c + gi[..., 1:2] * o_s + gi[..., 2:3] * o_w

    o = lax.map(block, (qb, gb, jnp.arange(nqb)))
    return o.transpose(1, 0, 2, 3, 4, 5).reshape(B, T, NSA_QW)


def setup_inputs(seed: int = 0) -> dict:
    key = jax.random.key(seed)
    ks = jax.random.split(key, 24)
    n = lambda k, shape, s: jax.random.normal(k, shape, jnp.float32) * s
    gain = lambda k, shape: 1.0 + 0.02 * jax.random.normal(k, shape, jnp.float32)
    L = DEPTH
    return {
        "x": n(ks[0], (BATCH, SEQ, D_MODEL), 1.0),
        "positions": jnp.broadcast_to(jnp.arange(SEQ, dtype=jnp.int32), (BATCH, SEQ)),
        "ffn1_norm": gain(ks[1], (L, D_MODEL)),
        "ffn1_w_gu": n(ks[2], (L, D_MODEL, 2 * D_FF), D_MODEL ** -0.5),
        "ffn1_w_down": n(ks[3], (L, D_FF, D_MODEL), D_FF ** -0.5),
        "mix_norm": gain(ks[4], (L, D_MODEL)),
        "w_in": n(ks[5], (L, D_MODEL, IN_COLS), D_MODEL ** -0.5),
        "diff_lambda": n(ks[6], (L, 4, HEAD_DIM), 0.1),
        "cmp_pos": n(ks[7], (L, 2, CMP_BLOCK, HEAD_DIM), 0.02),
        "cmp_w1": n(ks[8], (L, 2, CMP_BLOCK * HEAD_DIM, CMP_HIDDEN), (CMP_BLOCK * HEAD_DIM) ** -0.5),
        "cmp_w2": n(ks[9], (L, 2, CMP_HIDDEN, HEAD_DIM), CMP_HIDDEN ** -0.5),
        "w_branch_a": n(ks[10], (L, DIFF_VW, D_MODEL), DIFF_VW ** -0.5),
        "w_branch_b": n(ks[11], (L, NSA_QW, D_MODEL), NSA_QW ** -0.5),
        "w_out": n(ks[12], (L, D_MODEL, D_MODEL), D_MODEL ** -0.5),
        "ffn2_norm": gain(ks[13], (L, D_MODEL)),
        "ffn2_w_gu": n(ks[14], (L, D_MODEL, 2 * D_FF), D_MODEL ** -0.5),
        "ffn2_w_down": n(ks[15], (L, D_FF, D_MODEL), D_FF ** -0.5),
        "final_norm": gain(ks[16], (D_MODEL,)),
    }


def reference(x, positions, ffn1_norm, ffn1_w_gu, ffn1_w_down, mix_norm, w_in, diff_lambda,
              cmp_pos, cmp_w1, cmp_w2, w_branch_a, w_branch_b, w_out,
              ffn2_norm, ffn2_w_gu, ffn2_w_down, final_norm):
    B, T, _ = x.shape
    split_points = np.cumsum(SPLIT_SIZES)[:-1].tolist()
    h = x
    for l in range(DEPTH):
        h = h + 0.5 * swiglu(rms_norm(h, ffn1_norm[l]), ffn1_w_gu[l], ffn1_w_down[l])
        u = rms_norm(h, mix_norm[l])
        (q_d, k_d, v_d, q_n, kc_raw, vc_raw, ks_, vs_, kw_, vw_, g_n, g_a, g_b) = jnp.split(u @ w_in[l], split_points, axis=-1)
        q_d = rope_partial(q_d.reshape(B, T, DIFF_HEADS * 2, HEAD_DIM), positions).reshape(B, T, DIFF_HEADS, 2, HEAD_DIM)
        k_d = rope_partial(k_d.reshape(B, T, DIFF_HEADS * 2, HEAD_DIM), positions).reshape(B, T, DIFF_HEADS, 2, HEAD_DIM)
        v_d = v_d.reshape(B, T, DIFF_HEADS, DIFF_V_HEAD)
        lam_init = 0.8 - 0.6 * math.exp(-0.3 * l)
        lp = diff_lambda[l].astype(jnp.float32)
        lam = jnp.exp(jnp.sum(lp[0] * lp[1])) - jnp.exp(jnp.sum(lp[2] * lp[3])) + lam_init
        o_a = diff_attention(q_d, k_d, v_d, lam, lam_init)
        kv_shape = (B, T, NSA_GROUPS, HEAD_DIM)
        q_n = rope_partial(q_n.reshape(B, T, NSA_HEADS, HEAD_DIM), positions)
        kc = compress(kc_raw.reshape(kv_shape), cmp_pos[l, 0], cmp_w1[l, 0], cmp_w2[l, 0])
        vc = compress(vc_raw.reshape(kv_shape), cmp_pos[l, 1], cmp_w1[l, 1], cmp_w2[l, 1])
        ks_r = rope_partial(ks_.reshape(kv_shape), positions)
        kw_r = rope_partial(kw_.reshape(kv_shape), positions)
        gates = jax.nn.sigmoid(g_n.reshape(B, T, NSA_HEADS, 3))
        o_b = nsa_attention(q_n, kc, vc, ks_r, vs_.reshape(kv_shape), kw_r, vw_.reshape(kv_shape), gates)
        y = jax.nn.sigmoid(g_a) * (o_a @ w_branch_a[l]) + jax.nn.sigmoid(g_b) * (o_b @ w_branch_b[l])
        h = h + y @ w_out[l]
        h = h + 0.5 * swiglu(rms_norm(h, ffn2_norm[l]), ffn2_w_gu[l], ffn2_w_down[l])
    return rms_norm(h, final_norm)
```

```python
import math
import numpy as np
import ml_dtypes
from contextlib import ExitStack
import concourse.bass as bass
import concourse.mybir as mybir
from concourse.bass_utils import run_bass_kernel_spmd

F32 = mybir.dt.float32
BF16 = mybir.dt.bfloat16
I32 = mybir.dt.int32
AF = mybir.ActivationFunctionType
ALU = mybir.AluOpType
AX = mybir.AxisListType

T = 4096
D = 1024
DFF = 2816
NFC = DFF // 128
DEPTH = 2
IN_COLS = 4888
EPS = 1e-6
NG = T // 512
NEG = -30000.0
ROPE_COLS = [(0, 512), (512, 512), (1536, 512), (2304, 128), (2560, 128)]
NPERM = 1792
QORDER = [7, 0, 6, 1, 5, 2, 4, 3]

COMPUTE = ("pe", "act", "dve", "pool")
NDMA_SEMS = 48
NSW_SEMS = 12


class Prog:
    def __init__(self, nc, stack):
        self.nc = nc
        self.ops = {e: [] for e in ("pe", "act", "dve", "pool", "sp")}
        self.cnt = {e: 0 for e in COMPUTE}
        self.sems = {}
        for e in COMPUTE:
            self.sems["e_" + e] = stack.enter_context(nc.semaphore("s_" + e))
        for i in range(NDMA_SEMS):
            self.sems["d%d" % i] = stack.enter_context(nc.semaphore("s_d%d" % i))
        self.dval = [0] * NDMA_SEMS
        self.dnext = 0
        self.dnext_sw = 0
        self.last_write = {}
        self.readers = {}
        self.waited = {e: {} for e in self.ops}
        self.nwaits = 0
        self.nops = 0

    def _deps(self, eng, reads, writes):
        need = {}

        def add(tok):
            if tok is None:
                return
            s, v = tok
            if need.get(s, 0) < v:
                need[s] = v
        for k in reads:
            add(self.last_write.get(k))
        for k in writes:
            add(self.last_write.get(k))
            for s, v in self.readers.get(k, {}).items():
                add((s, v))
        out = []
        for s, v in need.items():
            if eng == "pe" and s == "e_pe":
                continue
            if self.waited[eng].get(s, 0) >= v:
                continue
            self.waited[eng][s] = v
            out.append((s, v))
        return out

    def _commit(self, tok, reads, writes):
        s, v = tok
        for k in reads:
            r = self.readers.setdefault(k, {})
            if r.get(s, 0) < v:
                r[s] = v
        for k in writes:
            self.last_write[k] = tok
            self.readers[k] = {}

    def op(self, eng, fn, reads=(), writes=()):
        waits = self._deps(eng, reads, writes)
        self.cnt[eng] += 1
        tok = ("e_" + eng, self.cnt[eng])
        self.ops[eng].append((waits, fn, ("e_" + eng, 1)))
        self._commit(tok, reads, writes)
        self.nwaits += len(waits)
        self.nops += 1

    def dma(self, fn, reads=(), writes=(), q="sp"):
        if q == "pool":
            i = self.dnext_sw
            self.dnext_sw = (self.dnext_sw + 1) % NSW_SEMS
        else:
            i = NSW_SEMS + self.dnext
            self.dnext = (self.dnext + 1) % (NDMA_SEMS - NSW_SEMS)
        s = "d%d" % i
        waits = self._deps(q, reads, writes)
        if self.dval[i] > 0 and self.waited[q].get(s, 0) < self.dval[i]:
            self.waited[q][s] = self.dval[i]
            waits.append((s, self.dval[i]))
        self.dval[i] += 16
        tok = (s, self.dval[i])
        self.ops[q].append((waits, fn, (s, 16)))
        self._commit(tok, reads, writes)
        self.nwaits += len(waits)
        self.nops += 1

    def barrier(self):
        toks = []
        for i in range(NDMA_SEMS):
            if self.dval[i] > 0:
                toks.append(("d%d" % i, self.dval[i]))
        for e in COMPUTE:
            if self.cnt[e] > 0:
                toks.append(("e_" + e, self.cnt[e]))
        for e in self.ops:
            waits = []
            for s, v in toks:
                if e == "pe" and s == "e_pe":
                    continue
                if self.waited[e].get(s, 0) >= v:
                    continue
                self.waited[e][s] = v
                waits.append((s, v))
            if waits:
                self.ops[e].append((waits, None, None))
                self.nwaits += len(waits)
        self.last_write = {}
        self.readers = {}

    def emit(self):
        nc = self.nc
        P = self

        def replay(name, eng):
            for waits, fn, inc in P.ops[name]:
                for s, v in waits:
                    eng.wait_ge(P.sems[s], v)
                if fn is None:
                    continue
                ins = fn(eng)
                ins.then_inc(P.sems[inc[0]], inc[1])

        with nc.Block() as block:
            @block.tensor
            def _(eng):
                replay("pe", eng)

            @block.scalar
            def _(eng):
                replay("act", eng)

            @block.vector
            def _(eng):
                replay("dve", eng)

            @block.gpsimd
            def _(eng):
                replay("pool", eng)

            @block.sync
            def _(eng):
                replay("sp", eng)


def make_consts():
    bf = ml_dtypes.bfloat16
    c = {}
    c["ident_bf"] = np.eye(128, dtype=np.float32).astype(bf)
    c["ident_f"] = np.eye(128, dtype=np.float32)
    c["ones_bf"] = np.ones((128, 128), np.float32).astype(bf)
    half = 8
    inv = 500000.0 ** (-2.0 * np.arange(half, dtype=np.float64) / 16.0)
    rp = np.zeros((128, 2), np.float32)
    for p in range(128):
        d = p % 64
        if d < 16:
            rp[p, 0] = inv[d % 8] / (2 * np.pi)
            rp[p, 1] = -1.0 if d < 8 else 1.0
    c["ropep"] = rp
    p = np.arange(128)[:, None]
    f = np.arange(512)[None, :]
    masks = []
    for rel in range(4):
        masks.append(np.where(f - p >= rel * 128, 0.0, NEG))
    for rel in (-4, -3, -2, -1):
        masks.append(np.where(f - p < 512 + rel * 128, 0.0, NEG))
    for th in (31, -481, -993, -1505, -2017):
        masks.append(np.where(f - 16 * p - th >= 0, 0.0, NEG))
    c["masks"] = np.stack(masks, 1).astype(np.float32).astype(bf)
    j = np.arange(64)[:, None]
    k = np.arange(T)[None, :]
    c["E"] = (k // 64 == j).astype(np.float32).astype(bf)
    cs = np.arange(256) * 16
    ss = np.arange(64) * 64
    ov = np.clip(np.minimum(cs[:, None] + 32, ss[None, :] + 64) - np.maximum(cs[:, None], ss[None, :]), 0, None) / 32.0
    ov[255, :] = 0.0
    ov1 = np.concatenate([ov, np.ones((256, 1))], 1).astype(np.float32)
    c["ov1"] = ov1.reshape(2, 128, 65).transpose(1, 0, 2).copy().astype(bf)
    q = np.arange(T)[:, None]
    jj = np.arange(64)[None, :]
    cur = q // 64
    forced = (jj == 0) | (jj == cur) | (jj == cur - 1)
    future = (jj * 64) > q
    A = np.where(future | forced, 0.0, 1.0)
    Bm = np.where(future, -1.0, np.where(forced, 1.0e4, 0.0))
    AB = np.stack([A, Bm], 0).astype(np.float32)
    c["impAB"] = AB.reshape(2, 32, 128, 64).transpose(2, 0, 1, 3).copy()
    sg = np.zeros((24, 24, 64), np.float32)
    for r in range(24):
        sg[r, r, :] = 1.0
    c["selg"] = sg
    return c


CONST_SPECS = [("ident_bf", [128, 128], BF16), ("ident_f", [128, 128], F32), ("ones_bf", [128, 128], BF16),
               ("ropep", [128, 2], F32), ("masks", [128, 13, 512], BF16), ("E", [64, T], BF16),
               ("ov1", [128, 2, 65], BF16), ("impAB", [128, 2, 32, 64], F32), ("selg", [24, 24, 64], F32)]

WEIGHT_SPECS = [("ffn1_norm", [DEPTH, D]), ("ffn1_w_gu", [DEPTH, D, 2 * DFF]), ("ffn1_w_down", [DEPTH, DFF, D]),
                ("mix_norm", [DEPTH, D]), ("w_in", [DEPTH, D, IN_COLS]), ("w_in_perm", [DEPTH, D, NPERM]),
                ("diff_lambda", [DEPTH, 256]), ("cmp_pos", [DEPTH, 2, 32, 64]), ("cmp_w1", [DEPTH, 2, 2048, 256]),
                ("cmp_w2", [DEPTH, 2, 256, 64]), ("w_branch_a", [DEPTH, 512, D]), ("w_branch_b", [DEPTH, 512, D]),
                ("w_out", [DEPTH, D, D]), ("ffn2_norm", [DEPTH, D]), ("ffn2_w_gu", [DEPTH, D, 2 * DFF]),
                ("ffn2_w_down", [DEPTH, DFF, D]), ("final_norm", [1, D])]


class Ctx:
    pass


def build(upto="all", taps=()):
    nc = bass.Bass("TRN2", target_bir_lowering=False)
    C = Ctx()
    C.nc = nc
    di = lambda n, s, d: nc.dram_tensor(n, s, d, kind="ExternalInput").ap()
    C.x = di("x", [T, D], F32)
    C.pos = di("pos", [1, T], I32)
    C.w = {n: di(n, s, F32) for n, s in WEIGHT_SPECS}
    C.c = {n: di(n, s, d) for n, s, d in CONST_SPECS}
    C.out = nc.dram_tensor("out", [T, D], F32, kind="ExternalOutput").ap()

    def scr(n, s, d):
        kind = "ExternalOutput" if n in taps else "Internal"
        return nc.dram_tensor(n, s, d, kind=kind).ap()
    C.hbuf = scr("hbuf", [T, D], F32)
    C.ropeC = scr("ropeC", [128, T], F32)
    C.ropeS = scr("ropeS", [128, T], F32)
    C.QdT = scr("QdT", [512, T], BF16)
    C.KdT = scr("KdT", [512, T], BF16)
    C.Vd = scr("Vd", [T, 512], BF16)
    C.QnT = scr("QnT", [512, T], BF16)
    C.kcrT = scr("kcrT", [128, T], BF16)
    C.vcrT = scr("vcrT", [128, T], BF16)
    C.ksT = scr("ksT", [128, T], BF16)
    C.kwT = scr("kwT", [128, T], BF16)
    C.vs = scr("vs", [T, 128], BF16)
    C.vw = scr("vw", [T, 128], BF16)
    C.gntok = scr("gntok", [T, 24], F32)
    C.oatok = scr("oatok", [T, 512], BF16)
    C.obtok = scr("obtok", [T, 512], BF16)
    C.gaT = scr("gaT", [D, T], F32)
    C.gbT = scr("gbT", [D, T], F32)
    C.kcT = scr("kcT", [128, 256], BF16)
    C.vc = scr("vc", [2, 256, 64], BF16)
    C.oaT = scr("oaT", [512, T], BF16)
    C.obT = scr("obT", [512, T], BF16)

    order = ["rope"]
    for l in range(DEPTH):
        order += ["ffn1_%d" % l, "proj_%d" % l, "cmp_%d" % l, "diff_%d" % l, "nsa_%d" % l, "outp_%d" % l, "ffn2_%d" % l]
    if upto == "all":
        upto = order[-1]
    todo = order[:order.index(upto) + 1]

    with ExitStack() as st:
        P = Prog(nc, st)
        C.P = P
        for ph in todo:
            if ph == "rope":
                phase_rope(C)
            else:
                name, l = ph.split("_")
                l = int(l)
                if name == "ffn1":
                    phase_ffn(C, C.x if l == 0 else C.hbuf, C.hbuf, C.w["ffn1_w_gu"][l], C.w["ffn1_w_down"][l],
                              C.w["ffn1_norm"][l:l + 1, :], None)
                elif name == "proj":
                    phase_proj(C, l)
                elif name == "cmp":
                    phase_cmp(C, l)
                elif name == "diff":
                    phase_diff(C, l)
                elif name == "nsa":
                    phase_nsa(C, l)
                elif name == "outp":
                    phase_outp(C, l)
                elif name == "ffn2":
                    last = (l == DEPTH - 1)
                    phase_ffn(C, C.hbuf, C.hbuf, C.w["ffn2_w_gu"][l], C.w["ffn2_w_down"][l],
                              C.w["ffn2_norm"][l:l + 1, :], C.w["final_norm"][0:1, :] if last else None)
            P.barrier()
        P.emit()
    C.nops = P.nops
    C.nwaits = P.nwaits
    return nc, C


def _alloc(C, st):
    nc = C.nc
    C.pid = getattr(C, "pid", 0) + 1
    sfx = "_%d" % C.pid
    sb = lambda n, s, d: st.enter_context(nc.sbuf_tensor(n + sfx, s, d))
    ps = lambda n, s, d: st.enter_context(nc.psum_tensor(n + sfx, s, d))
    return sb, ps


def load_const(C, tile_ap, name, key):
    C.P.dma(lambda e: e.dma_start(out=tile_ap, in_=C.c[name]), writes=[key])


def norm_stats(C, pfx, hin, ss, rstd, junk, hkey=None):
    P = C.P
    hkey = hkey or (pfx + "hin")
    for tt in range(4):
        P.op("act", lambda e, tt=tt: e.activation(junk[:], hin[:, tt, :], AF.Square, accum_out=ss[:, tt:tt + 1]),
             reads=[(hkey, tt)], writes=[pfx + "junk", (pfx + "ss", tt)])
    P.op("dve", lambda e: e.tensor_scalar(rstd[:], ss[:], 1.0 / D, EPS, ALU.mult, ALU.add),
         reads=[(pfx + "ss", tt) for tt in range(4)], writes=[pfx + "rstd"])
    P.op("act", lambda e: e.sqrt(rstd[:], rstd[:]), reads=[pfx + "rstd"], writes=[pfx + "rstd"])
    P.op("dve", lambda e: e.reciprocal(rstd[:], rstd[:]), reads=[pfx + "rstd"], writes=[pfx + "rstd"])


def norm_tile(C, pfx, tt, hin, gain, ident, rstd, xn, pt, xnT, hkey=None, xkey=None):
    P = C.P
    b = tt % 2
    hkey = hkey or (pfx + "hin")
    xkey = xkey or (pfx + "xnT")
    P.op("dve", lambda e: e.scalar_tensor_tensor(xn[b][:], hin[:, tt, :], rstd[:, tt:tt + 1], gain[:], ALU.mult, ALU.mult),
         reads=[(hkey, tt), pfx + "rstd", pfx + "gain"], writes=[(pfx + "xn", b)])

    def tr(e):
        ins = None
        for kc in range(8):
            ins = e.transpose(pt[:, kc, :], xn[b][:, kc * 128:(kc + 1) * 128], ident[:])
        return ins
    P.op("pe", tr, reads=[(pfx + "xn", b), pfx + "ident"], writes=[pfx + "pt"])
    P.op("act", lambda e: e.copy(xnT[:, :, tt * 128:(tt + 1) * 128], pt[:, :, :]),
         reads=[pfx + "pt"], writes=[(xkey, tt)])


def norm_group(C, pfx, hin, gain, ident, ss, rstd, junk, xn, pt, xnT):
    norm_stats(C, pfx, hin, ss, rstd, junk)
    for tt in range(4):
        norm_tile(C, pfx, tt, hin, gain, ident, rstd, xn, pt, xnT)


def phase_ffn(C, src, dst, wgu_d, wd_d, gain_d, fgain_d):
    P, nc = C.P, C.nc
    with ExitStack() as st:
        sb, ps = _alloc(C, st)
        wgu = sb("f_wgu", [128, 8, 2 * DFF], BF16)
        wd = sb("f_wd", [128, NFC, D], BF16)
        gain = sb("f_gain", [128, D], F32)
        fgain = sb("f_fgain", [128, D], F32) if fgain_d is not None else None
        ident = sb("f_ident", [128, 128], BF16)
        hin = sb("f_hin", [128, 4, D], F32)
        junk = sb("f_junk", [128, D], BF16)
        xn = [sb("f_xn%d" % i, [128, D], BF16) for i in range(2)]
        xnT = sb("f_xnT", [128, 8, 512], BF16)
        hh = sb("f_hh", [128, NFC, 512], BF16)
        sg = [sb("f_sg%d" % i, [128, 512], BF16) for i in range(2)]
        hout = [sb("f_hout%d" % i, [128, D], F32) for i in range(2)]
        oute = [sb("f_oute0", [128, D], F32)] * 2 if fgain_d is not None else None
        ss = sb("f_ss", [128, 4], F32)
        rstd = sb("f_rstd", [128, 4], F32)
        ss2 = sb("f_ss2", [128, 2], F32)
        rstd2 = sb("f_rstd2", [128, 2], F32)
        pg = [ps("f_pg%d" % i, [128, 512], F32) for i in range(2)]
        pu = [ps("f_pu%d" % i, [128, 512], F32) for i in range(2)]
        po = [ps("f_po%d" % i, [128, 512], F32) for i in range(2)]
        pt = ps("f_pt", [128, 8, 128], BF16)

        load_const(C, ident[:], "ident_bf", "f_ident")
        P.dma(lambda e: e.dma_start(out=gain[:], in_=gain_d.partition_broadcast(128)), writes=["f_gain"])
        if fgain_d is not None:
            P.dma(lambda e: e.dma_start(out=fgain[:], in_=fgain_d.partition_broadcast(128)), writes=["f_fgain"])
        FCH = [(0, 6), (6, 12), (12, 17), (17, 22)]
        for ci, (fa, fb) in enumerate(FCH):
            for hf in range(2):
                ca, cb = hf * DFF + fa * 128, hf * DFF + fb * 128
                P.dma(lambda e, ca=ca, cb=cb: e.dma_start(out=wgu[:, :, ca:cb], in_=wgu_d[:, ca:cb].rearrange("(kc p) f -> p kc f", p=128)),
                      writes=[("f_wgu", hf, ci)], q="pool")
        fc2ci = {}
        for ci, (fa, fb) in enumerate(FCH):
            for fc in range(fa, fb):
                fc2ci[fc] = ci
        for i in range(2):
            P.dma(lambda e, i=i: e.dma_start(out=wd[:, i * 11:(i + 1) * 11, :],
                                             in_=wd_d[i * 1408:(i + 1) * 1408, :].rearrange("(c p) f -> p c f", p=128)),
                  writes=[("f_wd", i)], q="pool")
        wd_keys = [("f_wd", 0), ("f_wd", 1)]
        pipe = fgain_d is None
        res = [sb("f_res%d" % i, [128, D], F32) for i in range(2)] if pipe else None

        def load_stats(g):
            r0 = g * 512
            P.dma(lambda e: e.dma_start(out=hin[:], in_=src[r0:r0 + 512, :].rearrange("(t p) d -> p t d", p=128)),
                  reads=[("hdram", g, tt) for tt in range(4)], writes=[("f_hin", tt) for tt in range(4)])
            norm_stats(C, "f_", hin, ss, rstd, junk)

        for g in range(NG):
            r0 = g * 512
            if not pipe or g == 0:
                load_stats(g)
                for tt in range(4):
                    norm_tile(C, "f_", tt, hin, gain, ident, rstd, xn, pt, xnT)
            xk = [("f_xnT", tt) for tt in range(4)]
            for fc in range(NFC):
                b = fc % 2

                def mm(e, col0, dstp):
                    ins = None
                    for kc in range(8):
                        ins = e.matmul(dstp[:], wgu[:, kc, col0:col0 + 128], xnT[:, kc, :], start=(kc == 0), stop=(kc == 7))
                    return ins
                P.op("pe", lambda e, fc=fc, b=b: mm(e, fc * 128, pg[b]), reads=xk + [("f_wgu", 0, fc2ci[fc])], writes=[("f_pg", b)])
                P.op("pe", lambda e, fc=fc, b=b: mm(e, DFF + fc * 128, pu[b]), reads=xk + [("f_wgu", 1, fc2ci[fc])], writes=[("f_pu", b)])
                P.op("act", lambda e, b=b: e.activation(sg[b][:], pg[b][:], AF.Silu), reads=[("f_pg", b)], writes=[("f_sg", b)])
                P.op("dve", lambda e, fc=fc, b=b: e.tensor_tensor(hh[:, fc, :], sg[b][:], pu[b][:], ALU.mult),
                     reads=[("f_sg", b), ("f_pu", b)], writes=[("f_hh", fc)])
            hk = [("f_hh", fc) for fc in range(NFC)]
            if pipe and g + 1 < NG:
                load_stats(g + 1)
            for tt in range(4):
                hb = tt % 2
                if pipe:
                    P.dma(lambda e, tt=tt, hb=hb, r0=r0: e.dma_start(out=res[hb][:], in_=src[r0 + tt * 128:r0 + (tt + 1) * 128, :]),
                          reads=[("hdram", g, tt)], writes=[("f_res", hb)])
                for half in range(2):
                    b = half

                    def mmd(e, tt=tt, half=half, b=b):
                        ins = None
                        for fc in range(NFC):
                            ins = e.matmul(po[b][:], hh[:, fc, tt * 128:(tt + 1) * 128], wd[:, fc, half * 512:(half + 1) * 512],
                                           start=(fc == 0), stop=(fc == NFC - 1))
                        return ins
                    P.op("pe", mmd, reads=hk + wd_keys, writes=[("f_po", b)])
                    if pipe:
                        P.op("dve", lambda e, half=half, b=b, hb=hb: e.scalar_tensor_tensor(
                            hout[hb][:, half * 512:(half + 1) * 512], po[b][:], 0.5, res[hb][:, half * 512:(half + 1) * 512],
                            ALU.mult, ALU.add), reads=[("f_po", b), ("f_res", hb)], writes=[("f_hout", hb, half)])
                    else:
                        P.op("dve", lambda e, tt=tt, half=half, b=b, hb=hb: e.scalar_tensor_tensor(
                            hout[hb][:, half * 512:(half + 1) * 512], po[b][:], 0.5, hin[:, tt, half * 512:(half + 1) * 512],
                            ALU.mult, ALU.add), reads=[("f_po", b), ("f_hin", tt)], writes=[("f_hout", hb, half)])
                hkeys = [("f_hout", hb, 0), ("f_hout", hb, 1)]
                rr = r0 + tt * 128
                if fgain_d is None:
                    P.dma(lambda e, rr=rr, hb=hb: e.dma_start(out=dst[rr:rr + 128, :], in_=hout[hb][:]),
                          reads=hkeys, writes=[("hdram", g, tt)])
                    if pipe and g + 1 < NG:
                        norm_tile(C, "f_", tt, hin, gain, ident, rstd, xn, pt, xnT)
                else:
                    P.op("act", lambda e, hb=hb: e.activation(junk[:], hout[hb][:], AF.Square, accum_out=ss2[:, hb:hb + 1]),
                         reads=hkeys, writes=["f_junk", ("f_ss2", hb)])
                    P.op("dve", lambda e, hb=hb: e.tensor_scalar(rstd2[:, hb:hb + 1], ss2[:, hb:hb + 1], 1.0 / D, EPS, ALU.mult, ALU.add),
                         reads=[("f_ss2", hb)], writes=[("f_rstd2", hb)])
                    P.op("act", lambda e, hb=hb: e.sqrt(rstd2[:, hb:hb + 1], rstd2[:, hb:hb + 1]),
                         reads=[("f_rstd2", hb)], writes=[("f_rstd2", hb)])
                    P.op("dve", lambda e, hb=hb: e.reciprocal(rstd2[:, hb:hb + 1], rstd2[:, hb:hb + 1]),
                         reads=[("f_rstd2", hb)], writes=[("f_rstd2", hb)])
                    P.op("dve", lambda e, hb=hb: e.scalar_tensor_tensor(oute[hb][:], hout[hb][:], rstd2[:, hb:hb + 1], fgain[:],
                                                                        ALU.mult, ALU.mult),
                         reads=hkeys + [("f_rstd2", hb), "f_fgain"], writes=["f_oute"])
                    P.dma(lambda e, rr=rr, hb=hb: e.dma_start(out=C.out[rr:rr + 128, :], in_=oute[hb][:]),
                          reads=["f_oute"], writes=[("outdram", g, tt)])


def phase_rope(C):
    P, nc = C.P, C.nc
    with ExitStack() as st:
        sb, ps = _alloc(C, st)
        posi = sb("r_posi", [128, T], I32)
        r = sb("r_r", [128, T], F32)
        ki = sb("r_ki", [128, T], I32)
        kf = sb("r_kf", [128, T], F32)
        f = sb("r_f", [128, T], F32)
        m = sb("r_m", [128, T], F32)
        rp = sb("r_rp", [128, 2], F32)
        load_const(C, rp[:], "ropep", "r_rp")
        P.dma(lambda e: e.dma_start(out=posi[:], in_=C.pos.partition_broadcast(128)), writes=["r_posi"])
        for which in range(2):
            P.op("dve", lambda e: e.tensor_copy(r[:], posi[:]), reads=["r_posi"], writes=["r_r"])
            if which == 0:
                P.op("dve", lambda e: e.tensor_scalar(r[:], r[:], rp[:, 0:1], None, ALU.mult), reads=["r_r", "r_rp"], writes=["r_r"])
            else:
                P.op("dve", lambda e: e.tensor_scalar(r[:], r[:], rp[:, 0:1], 0.25, ALU.mult, ALU.add), reads=["r_r", "r_rp"], writes=["r_r"])
            P.op("dve", lambda e: e.tensor_copy(ki[:], r[:]), reads=["r_r"], writes=["r_ki"])
            P.op("dve", lambda e: e.tensor_copy(kf[:], ki[:]), reads=["r_ki"], writes=["r_kf"])
            P.op("dve", lambda e: e.tensor_tensor(f[:], r[:], kf[:], ALU.subtract), reads=["r_r", "r_kf"], writes=["r_f"])
            P.op("dve", lambda e: e.tensor_single_scalar(m[:], f[:], 0.5, ALU.is_gt), reads=["r_f"], writes=["r_m"])
            P.op("dve", lambda e: e.tensor_tensor(f[:], f[:], m[:], ALU.subtract), reads=["r_f", "r_m"], writes=["r_f"])
            P.op("dve", lambda e: e.tensor_single_scalar(m[:], f[:], -0.5, ALU.is_lt), reads=["r_f"], writes=["r_m"])
            P.op("dve", lambda e: e.tensor_tensor(f[:], f[:], m[:], ALU.add), reads=["r_f", "r_m"], writes=["r_f"])
            P.op("act", lambda e: e.activation(kf[:], f[:], AF.Sin, scale=6.28318), reads=["r_f"], writes=["r_kf"])
            if which == 0:
                P.op("dve", lambda e: e.tensor_scalar(kf[:], kf[:], rp[:, 1:2], None, ALU.mult), reads=["r_kf", "r_rp"], writes=["r_kf"])
                P.dma(lambda e: e.dma_start(out=C.ropeS, in_=kf[:]), reads=["r_kf"], writes=["ropeS"])
            else:
                P.dma(lambda e: e.dma_start(out=C.ropeC, in_=kf[:]), reads=["r_kf"], writes=["ropeC"])


def _proj_blocks():
    blks = []
    for i in range(4):
        blks.append((i * 128, 128, i * 128, 0.125, "rope", "QdT", i * 128))
    for i in range(4):
        blks.append((512 + i * 128, 128, 512 + i * 128, 1.0, "rope", "KdT", i * 128))
    for i in range(4):
        blks.append((1536 + i * 128, 128, 1024 + i * 128, 0.125, "rope", "QnT", i * 128))
    blks.append((2048, 128, None, 1.0, "copy", "kcrT", 0))
    blks.append((2176, 128, None, 1.0, "copy", "vcrT", 0))
    blks.append((2304, 128, 1536, 1.0, "rope", "ksT", 0))
    blks.append((2560, 128, 1664, 1.0, "rope", "kwT", 0))
    for i in range(8):
        blks.append((2840 + i * 128, 128, None, 1.0, "sig", "gaT", i * 128))
    for i in range(8):
        blks.append((3864 + i * 128, 128, None, 1.0, "sig", "gbT", i * 128))
    return blks


def phase_proj(C, l):
    P, nc = C.P, C.nc
    win_d = C.w["w_in"][l]
    wpm_d = C.w["w_in_perm"][l]
    with ExitStack() as st:
        sb, ps = _alloc(C, st)
        win = sb("p_win", [128, 8, IN_COLS], BF16)
        wpm = sb("p_wpm", [128, 8, NPERM], BF16)
        gain = sb("p_gain", [128, D], F32)
        ident = sb("p_ident", [128, 128], BF16)
        hin2 = [sb("p_hin%d" % i, [128, 4, D], F32) for i in range(2)]
        junk = sb("p_junk", [128, D], BF16)
        xn = [sb("p_xn%d" % i, [128, D], BF16) for i in range(2)]
        uT2 = [sb("p_uT%d" % i, [128, 8, 512], BF16) for i in range(2)]
        ss = sb("p_ss", [128, 4], F32)
        rstd = sb("p_rstd", [128, 4], F32)
        Cs2 = [sb("p_C%d" % i, [128, 512], F32) for i in range(2)]
        Ss2 = [sb("p_S%d" % i, [128, 512], F32) for i in range(2)]
        t1 = [sb("p_t1%d" % i, [128, 512], F32) for i in range(2)]
        t2 = [sb("p_t2%d" % i, [128, 512], F32) for i in range(2)]
        stb = [sb("p_stb%d" % i, [128, 512], BF16) for i in range(3)]
        stf = [sb("p_stf%d" % i, [128, 512], F32) for i in range(3)]
        pa = [ps("p_pa%d" % i, [128, 512], F32) for i in range(2)]
        pb = [ps("p_pb%d" % i, [128, 512], F32) for i in range(2)]
        pv = [ps("p_pv%d" % i, [128, 512], F32) for i in range(2)]
        pt = ps("p_pt", [128, 8, 128], BF16)
        load_const(C, ident[:], "ident_bf", "p_ident")
        P.dma(lambda e: e.dma_start(out=gain[:], in_=C.w["mix_norm"][l:l + 1, :].partition_broadcast(128)), writes=["p_gain"])
        WCH = [(0, 512), (512, 1024), (1024, 1536), (1536, 2048), (2048, 2840), (2840, 3864), (3864, 4888)]
        PCH = [(0, 512), (512, 1024), (1024, 1536), (1536, 1792)]

        def ld(t, d, lo, hi, key):
            P.dma(lambda e: e.dma_start(out=t[:, :, lo:hi], in_=d[:, lo:hi].rearrange("(kc p) f -> p kc f", p=128)),
                  writes=[key], q="pool")
        for kind, ci in (("w", 0), ("p", 0), ("w", 1), ("p", 1), ("w", 3), ("p", 2), ("w", 4), ("p", 3), ("w", 5), ("w", 6), ("w", 2)):
            if kind == "w":
                ld(win, win_d, WCH[ci][0], WCH[ci][1], ("p_win", ci))
            else:
                ld(wpm, wpm_d, PCH[ci][0], PCH[ci][1], ("p_wpm", ci))

        def wkey(c0):
            return [("p_win", [i for i, (a, b_) in enumerate(WCH) if a <= c0 < b_][0])]

        def pkey(c0):
            return [("p_wpm", [i for i, (a, b_) in enumerate(PCH) if a <= c0 < b_][0])]
        wk = [("p_win", i) for i in range(len(WCH))]
        blks = _proj_blocks()
        nb = 0
        nf = 0
        def load_stats(g):
            nb_ = g % 2
            r0_ = g * 512
            P.dma(lambda e: e.dma_start(out=hin2[nb_][:], in_=C.hbuf[r0_:r0_ + 512, :].rearrange("(t p) d -> p t d", p=128)),
                  reads=[("hdram", g)], writes=[("p_hin%d" % nb_, tt) for tt in range(4)])
            P.dma(lambda e: e.dma_start(out=Cs2[nb_][:], in_=C.ropeC[:, r0_:r0_ + 512]), reads=["ropeC"], writes=[("p_C", nb_)])
            P.dma(lambda e: e.dma_start(out=Ss2[nb_][:], in_=C.ropeS[:, r0_:r0_ + 512]), reads=["ropeS"], writes=[("p_S", nb_)])
            norm_stats(C, "p_", hin2[nb_], ss, rstd, junk, hkey="p_hin%d" % nb_)

        def ntile(g, tt):
            nb_ = g % 2
            norm_tile(C, "p_", tt, hin2[nb_], gain, ident, rstd, xn, pt, uT2[nb_], hkey="p_hin%d" % nb_, xkey="p_xnT%d" % nb_)

        for g in range(NG):
            r0 = g * 512
            gb_ = g % 2
            uT, Cs, Ss = uT2[gb_], Cs2[gb_], Ss2[gb_]
            if g == 0:
                load_stats(0)
                for tt in range(4):
                    ntile(0, tt)
            xk = [("p_xnT%d" % gb_, tt) for tt in range(4)]
            for bi, (c0, ncol, pc0, scale, kind, dname, drow) in enumerate(blks):
                b = bi % 2
                dst = getattr(C, dname)
                if g + 1 < NG:
                    if bi == 16:
                        load_stats(g + 1)
                    if bi in (18, 22, 26, 30):
                        ntile(g + 1, (bi - 18) // 4)

                def mm(e, wt, col0, ncol, dstp, uT=uT):
                    ins = None
                    for kc in range(8):
                        ins = e.matmul(dstp[0:ncol, :], wt[:, kc, col0:col0 + ncol], uT[:, kc, :], start=(kc == 0), stop=(kc == 7))
                    return ins
                P.op("pe", lambda e, c0=c0, ncol=ncol, b=b, mm=mm: mm(e, win, c0, ncol, pa[b]), reads=xk + wkey(c0), writes=[("p_pa", b)])
                if kind == "rope":
                    P.op("pe", lambda e, pc0=pc0, ncol=ncol, b=b, mm=mm: mm(e, wpm, pc0, ncol, pb[b]), reads=xk + pkey(pc0), writes=[("p_pb", b)])
                    P.op("dve", lambda e, b=b, scale=scale, Cs=Cs: e.scalar_tensor_tensor(t1[b][:], pa[b][:], scale, Cs[:], ALU.mult, ALU.mult),
                         reads=[("p_pa", b), ("p_C", gb_)], writes=[("p_t1", b)])
                    P.op("dve", lambda e, b=b, scale=scale, Ss=Ss: e.scalar_tensor_tensor(t2[b][:], pb[b][:], scale, Ss[:], ALU.mult, ALU.mult),
                         reads=[("p_pb", b), ("p_S", gb_)], writes=[("p_t2", b)])
                    sbi = nb % 3
                    nb += 1
                    P.op("pool", lambda e, b=b, sbi=sbi: e.tensor_tensor(stb[sbi][:], t1[b][:], t2[b][:], ALU.add),
                         reads=[("p_t1", b), ("p_t2", b)], writes=[("p_stb", sbi)])
                    P.dma(lambda e, dst=dst, drow=drow, r0=r0, sbi=sbi: e.dma_start(out=dst[drow:drow + 128, r0:r0 + 512], in_=stb[sbi][:]),
                          reads=[("p_stb", sbi)], writes=[(dname, drow, g)])
                elif kind == "copy":
                    sbi = nb % 3
                    nb += 1
                    P.op("act", lambda e, b=b, sbi=sbi: e.copy(stb[sbi][:], pa[b][:]), reads=[("p_pa", b)], writes=[("p_stb", sbi)])
                    P.dma(lambda e, dst=dst, drow=drow, r0=r0, sbi=sbi: e.dma_start(out=dst[drow:drow + 128, r0:r0 + 512], in_=stb[sbi][:]),
                          reads=[("p_stb", sbi)], writes=[(dname, drow, g)])
                else:
                    sfi = nf % 3
                    nf += 1
                    P.op("act", lambda e, b=b, sfi=sfi, ncol=ncol: e.activation(stf[sfi][0:ncol, :], pa[b][0:ncol, :], AF.Sigmoid),
                         reads=[("p_pa", b)], writes=[("p_stf", sfi)])
                    P.dma(lambda e, dst=dst, drow=drow, r0=r0, sfi=sfi, ncol=ncol: e.dma_start(out=dst[drow:drow + ncol, r0:r0 + 512], in_=stf[sfi][0:ncol, :]),
                          reads=[("p_stf", sfi)], writes=[(dname, drow, g)])
            for tt in range(4):
                b = tt % 2
                rr = r0 + tt * 128

                def mmv(e, col0, ncol, dstp, tt=tt, uT=uT):
                    ins = None
                    for kc in range(8):
                        ins = e.matmul(dstp, uT[:, kc, tt * 128:(tt + 1) * 128], win[:, kc, col0:col0 + ncol], start=(kc == 0), stop=(kc == 7))
                    return ins
                P.op("pe", lambda e, b=b, mmv=mmv: mmv(e, 1024, 512, pv[b][:, :]), reads=xk + wk, writes=[("p_pv", b)])
                sbi = nb % 3
                nb += 1
                P.op("act", lambda e, b=b, sbi=sbi: e.copy(stb[sbi][:], pv[b][:]), reads=[("p_pv", b)], writes=[("p_stb", sbi)])
                P.dma(lambda e, rr=rr, sbi=sbi: e.dma_start(out=C.Vd[rr:rr + 128, :], in_=stb[sbi][:]),
                      reads=[("p_stb", sbi)], writes=[("Vd", g, tt)])

                def mm2(e, b=b, mmv=mmv):
                    mmv(e, 2432, 128, pb[b][:, 0:128])
                    mmv(e, 2688, 128, pb[b][:, 128:256])
                    return mmv(e, 2816, 24, pb[b][:, 256:280])
                P.op("pe", mm2, reads=xk + wk, writes=[("p_pb", b)])
                sbi = nb % 3
                nb += 1
                P.op("act", lambda e, b=b, sbi=sbi: e.copy(stb[sbi][:, 0:256], pb[b][:, 0:256]), reads=[("p_pb", b)], writes=[("p_stb", sbi)])
                P.dma(lambda e, rr=rr, sbi=sbi: e.dma_start(out=C.vs[rr:rr + 128, :], in_=stb[sbi][:, 0:128]),
                      reads=[("p_stb", sbi)], writes=[("vs", g, tt)])
                P.dma(lambda e, rr=rr, sbi=sbi: e.dma_start(out=C.vw[rr:rr + 128, :], in_=stb[sbi][:, 128:256]),
                      reads=[("p_stb", sbi)], writes=[("vw", g, tt)])
                sfi = nf % 3
                nf += 1
                P.op("act", lambda e, b=b, sfi=sfi: e.activation(stf[sfi][:, 0:24], pb[b][:, 256:280], AF.Sigmoid),
                     reads=[("p_pb", b)], writes=[("p_stf", sfi)])
                P.dma(lambda e, rr=rr, sfi=sfi: e.dma_start(out=C.gntok[rr:rr + 128, :], in_=stf[sfi][:, 0:24]),
                      reads=[("p_stf", sfi)], writes=[("gntok", g, tt)])


def phase_cmp(C, l):
    P, nc = C.P, C.nc
    with ExitStack() as st:
        sb, ps = _alloc(C, st)
        raw = [sb("c_raw%d" % i, [128, T], BF16) for i in range(2)]
        w1 = [sb("c_w1%d" % i, [128, 32, 256], BF16) for i in range(2)]
        w2 = [sb("c_w2%d" % i, [128, 2, 64], BF16) for i in range(2)]
        posT = [sb("c_posT%d" % i, [128, 32], BF16) for i in range(2)]
        bias = [sb("c_bias%d" % i, [128, 2], F32) for i in range(2)]
        hT = [sb("c_hT%d" % i, [128, 2, 256], BF16) for i in range(2)]
        ko = [sb("c_ko%d" % i, [128, 256], BF16) for i in range(2)]
        vo = [sb("c_vo%d" % i, [128, 64], BF16) for i in range(2)]
        pbias = ps("c_pbias", [128, 2], F32)
        ph = [ps("c_ph%d" % i, [128, 256], F32) for i in range(2)]
        pk = ps("c_pk", [128, 256], F32)
        pvv = [ps("c_pvv%d" % i, [128, 64], F32) for i in range(2)]
        P.dma(lambda e: e.dma_start(out=raw[0][:], in_=C.kcrT), reads=["kcrT_all"], writes=[("c_raw", 0)])
        P.dma(lambda e: e.dma_start(out=raw[1][:], in_=C.vcrT), reads=["vcrT_all"], writes=[("c_raw", 1)])
        for kv in range(2):
            w1_d = C.w["cmp_w1"][l, kv].rearrange("(l d) h -> d l h", d=64)
            for hf in range(2):
                P.dma(lambda e, kv=kv, hf=hf, w1_d=w1_d: e.dma_start(out=w1[kv][hf * 64:(hf + 1) * 64, :, :], in_=w1_d),
                      writes=[("c_w1", kv, hf)], q="pool")
                P.dma(lambda e, kv=kv, hf=hf: e.dma_start(out=posT[kv][hf * 64:(hf + 1) * 64, :],
                                                         in_=C.w["cmp_pos"][l, kv].rearrange("l d -> d l"),
                                                         allow_slow_non_contiguous=True),
                      writes=[("c_posT", kv, hf)], q="pool")
            P.dma(lambda e, kv=kv: e.dma_start(out=w2[kv][:], in_=C.w["cmp_w2"][l, kv].rearrange("(c p) o -> p c o", p=128)),
                  writes=[("c_w2", kv)], q="pool")
            P.op("dve", lambda e, kv=kv: e.memset(hT[kv][:], 0.0), writes=[("c_hT", kv)])
        cnt = 0
        for kv in range(2):
            def fb(e, kv=kv):
                ins = None
                for hc in range(2):
                    for li in range(32):
                        ins = e.matmul(pbias[:, hc:hc + 1], w1[kv][0:64, li, hc * 128:(hc + 1) * 128], posT[kv][0:64, li:li + 1],
                                       start=(li == 0), stop=(li == 31))
                return ins
            P.op("pe", fb, reads=[("c_w1", kv, 0), ("c_posT", kv, 0)], writes=["c_pbias"])
            P.op("dve", lambda e, kv=kv: e.tensor_copy(bias[kv][:], pbias[:]), reads=["c_pbias"], writes=[("c_bias", kv)])
            for g in range(2):
                lo, hi = g * 64, (g + 1) * 64
                for hc in range(2):
                    def fh(e, kv=kv, hc=hc, lo=lo, hi=hi):
                        ins = None
                        for li in range(32):
                            ins = e.matmul(ph[hc][:, 0:255], w1[kv][lo:hi, li, hc * 128:(hc + 1) * 128],
                                           raw[kv][lo:hi, li:li + 4065:16], start=(li == 0), stop=(li == 31))
                        return ins
                    P.op("pe", fh, reads=[("c_w1", kv, g), ("c_raw", kv)], writes=[("c_ph", hc)])
                    P.op("act", lambda e, kv=kv, hc=hc: e.activation(hT[kv][:, hc, 0:255], ph[hc][:, 0:255], AF.Silu,
                                                                     bias=bias[kv][:, hc:hc + 1]),
                         reads=[("c_ph", hc), ("c_bias", kv)], writes=[("c_hT", kv)])
                if kv == 0:
                    def fk(e):
                        ins = None
                        for hc in range(2):
                            ins = e.matmul(pk[0:64, :], w2[0][:, hc, :], hT[0][:, hc, :], start=(hc == 0), stop=(hc == 1))
                        return ins
                    P.op("pe", fk, reads=[("c_w2", 0), ("c_hT", 0)], writes=["c_pk"])
                    P.op("dve", lambda e, g=g: e.tensor_copy(ko[g][0:64, :], pk[0:64, :]), reads=["c_pk"], writes=[("c_ko", g)])
                    P.dma(lambda e, g=g, lo=lo, hi=hi: e.dma_start(out=C.kcT[lo:hi, :], in_=ko[g][0:64, :]),
                          reads=[("c_ko", g)], writes=[("kcT", g)])
                else:
                    for ct in range(2):
                        def fv(e, ct=ct):
                            ins = None
                            for hc in range(2):
                                ins = e.matmul(pvv[ct][:], hT[1][:, hc, ct * 128:(ct + 1) * 128], w2[1][:, hc, :],
                                               start=(hc == 0), stop=(hc == 1))
                            return ins
                        P.op("pe", fv, reads=[("c_w2", 1), ("c_hT", 1)], writes=[("c_pvv", ct)])
                        P.op("dve", lambda e, ct=ct: e.tensor_copy(vo[ct][:], pvv[ct][:]), reads=[("c_pvv", ct)], writes=[("c_vo", ct)])
                        P.dma(lambda e, g=g, ct=ct: e.dma_start(out=C.vc[g, ct * 128:(ct + 1) * 128, :], in_=vo[ct][:]),
                              reads=[("c_vo", ct)], writes=[("vc", g, ct)])


class AttnShared:
    pass


def attn_setup(C, sb, ps, pfx):
    A = AttnShared()
    A.S, A.O, A.SUM = [], [], []
    for i in range(2):
        A.S.append(ps(pfx + "S%d" % i, [128, 512], F32))
        A.O.append(ps(pfx + "O%d" % i, [128, 512], F32))
        A.SUM.append(ps(pfx + "SUM%d" % i, [128, 512], F32))
    A.Pt = [sb(pfx + "Pt%d" % i, [128, 512], BF16) for i in range(2)]
    A.ones = sb(pfx + "ones", [128, 128], BF16)
    A.masks = sb(pfx + "masks", [128, 13, 512], BF16)
    A.ident = sb(pfx + "identb", [128, 128], BF16)
    load_const(C, A.ones[:], "ones_bf", "a_ones")
    load_const(C, A.masks[:], "masks", "a_masks")
    load_const(C, A.ident[:], "ident_bf", "a_ident")
    A.base = 0
    A.calls = 0
    return A


def attn_call(C, A, qT, kT_fn, v_fn, dv, steps, rkeys, step_hook=None):
    P = C.P
    par = A.calls % 2
    A.calls += 1
    n = len(steps)
    base = A.base
    A.base += n

    def S_op(j):
        kt, masks, c0, c1 = steps[j]
        sp = (base + j) % 2

        def f(e):
            ins = e.matmul(A.S[sp][:, c0:c1], kT_fn(kt), qT[:, c0:c1], start=True, stop=(len(masks) == 0))
            for i, (lh, rh) in enumerate(masks):
                ins = e.matmul(A.S[sp][:, c0:c1], lh, rh[:, c0:c1], start=False, stop=(i == len(masks) - 1))
            return ins
        P.op("pe", f, reads=list(rkeys) + ["a_masks", "a_ident"], writes=[("a_S", sp)])

    def E_op(j):
        kt, masks, c0, c1 = steps[j]
        sp = (base + j) % 2
        P.op("act", lambda e: e.activation(A.Pt[sp][:, c0:c1], A.S[sp][:, c0:c1], AF.Exp), reads=[("a_S", sp)], writes=[("a_Pt", sp)])

    def PV_op(j):
        kt, _, c0, c1 = steps[j]
        sp = (base + j) % 2

        def f(e):
            e.matmul(A.O[par][0:dv, c0:c1], v_fn(kt), A.Pt[sp][:, c0:c1], start=(j == 0), stop=(j == n - 1))
            return e.matmul(A.SUM[par][0:dv, c0:c1], A.ones[:, 0:dv], A.Pt[sp][:, c0:c1], start=(j == 0), stop=(j == n - 1))
        P.op("pe", f, reads=list(rkeys) + [("a_Pt", sp), "a_ones"], writes=[("a_O", par), ("a_SUM", par)])
        if step_hook is not None:
            step_hook(j, sp)

    S_op(0)
    for j in range(n):
        E_op(j)
        if j + 1 < n:
            S_op(j + 1)
        PV_op(j)
    return par


def attn_setup2(C, sb, ps, pfx):
    A = AttnShared()
    A.S, A.OTA, A.OTB = [], [], []
    for i in range(2):
        A.S.append(ps(pfx + "S%d" % i, [128, 512], F32))
        A.OTA.append(ps(pfx + "OTA%d" % i, [128, 512], F32))
        A.OTB.append(ps(pfx + "OTB%d" % i, [128, 512], F32))
    A.S.append(ps(pfx + "S2", [128, 512], F32))
    A.Pt = [sb(pfx + "Pt%d" % i, [128, 512], BF16) for i in range(3)]
    A.masks = sb(pfx + "masks", [128, 13, 512], BF16)
    A.ident = sb(pfx + "identb", [128, 128], BF16)
    load_const(C, A.masks[:], "masks", "a_masks")
    load_const(C, A.ident[:], "ident_bf", "a_ident")
    A.base = 0
    A.calls = 0
    A.queue = []
    return A


def ot_ap(A, par, qt, wide, c0=0, c1=None):
    if not wide:
        return ("A", par), A.OTA[par], qt * 128
    bank = A.OTA[par] if qt < 2 else A.OTB[par]
    return ("A" if qt < 2 else "B", par), bank, (qt % 2) * 129


class Call:
    pass


def attn_call2(C, A, qT, kT_fn, vp_fn, nv, steps, rkeys, wide, epi=None):
    c = Call()
    c.qT, c.kT_fn, c.vp_fn, c.nv, c.steps, c.rkeys, c.wide, c.epi = qT, kT_fn, vp_fn, nv, steps, list(rkeys), wide, epi
    A.queue.append(c)
    return c


def emit_calls(C, A, look=2):
    P = C.P
    calls = A.queue
    A.queue = []
    NS, NP = len(A.S), len(A.Pt)
    flat = []
    for ci, c in enumerate(calls):
        c.par = A.calls % 2
        A.calls += 1
        writes = []
        for j in range(len(c.steps)):
            _, _, c0, c1 = c.steps[j]
            for qt in range(c0 // 128, c1 // 128):
                writes.append((j, qt, ot_ap(A, c.par, qt, c.wide)[0]))
        c.first, c.last = {}, {}
        for j, qt, b in writes:
            c.first.setdefault(b, (j, qt))
            c.last[b] = (j, qt)
        for j in range(len(c.steps)):
            flat.append((c, j))
    base = A.base
    A.base += len(flat)

    def S_op(i):
        c, j = flat[i]
        kt, masks, c0, c1 = c.steps[j]
        sp = (base + i) % NS

        def f(e):
            ins = e.matmul(A.S[sp][:, c0:c1], c.kT_fn(kt), c.qT[:, c0:c1], start=True, stop=(len(masks) == 0))
            for k, (lh, rh) in enumerate(masks):
                ins = e.matmul(A.S[sp][:, c0:c1], lh, rh[:, c0:c1], start=False, stop=(k == len(masks) - 1))
            return ins
        P.op("pe", f, reads=c.rkeys + ["a_masks", "a_ident"], writes=[("a_S", sp)])

    def E_op(i):
        c, j = flat[i]
        kt, masks, c0, c1 = c.steps[j]
        sp = (base + i) % NS
        pp = (base + i) % NP
        P.op("act", lambda e: e.activation(A.Pt[pp][:, c0:c1], A.S[sp][:, c0:c1], AF.Exp), reads=[("a_S", sp)], writes=[("a_Pt", pp)])

    def PV_op(i):
        c, j = flat[i]
        kt, _, c0, c1 = c.steps[j]
        pp = (base + i) % NP
        plan = []
        for qt in range(c0 // 128, c1 // 128):
            b, bank, off = ot_ap(A, c.par, qt, c.wide)
            plan.append((qt, bank, off, c.first[b] == (j, qt), c.last[b] == (j, qt)))
        nv = c.nv

        def f(e):
            ins = None
            for qt, bank, off, st_, sp_ in plan:
                ins = e.matmul(bank[:, off:off + nv], A.Pt[pp][:, qt * 128:(qt + 1) * 128], c.vp_fn(kt), start=st_, stop=sp_)
            return ins
        P.op("pe", f, reads=c.rkeys + [("a_Pt", pp)], writes=[("a_OT", c.par)])
        if j == len(c.steps) - 1 and c.epi is not None:
            c.epi(c.par)

    n = len(flat)
    for i in range(min(look, n)):
        S_op(i)
    for i in range(n):
        E_op(i)
        if i + look < n:
            S_op(i + look)
        PV_op(i)


def phase_diff(C, l):
    P, nc = C.P, C.nc
    lam_init = 0.8 - 0.6 * math.exp(-0.3 * l)
    with ExitStack() as st:
        sb, ps = _alloc(C, st)
        A = attn_setup2(C, sb, ps, "d_")
        Q = [[sb("d_Q%d_%d" % (i, m), [128, T], BF16) for m in range(2)] for i in range(2)]
        Kt = [sb("d_K%d" % i, [128, T], BF16) for i in range(2)]
        V = [sb("d_V%d" % i, [128, 32, 129], BF16) for i in range(2)]
        for i in range(2):
            P.op("pool", lambda e, i=i: e.memset(Q[i][0][64:128, :], 0.0), writes=[("d_Qz", i, 0)])
            P.op("pool", lambda e, i=i: e.memset(Q[i][1][0:64, :], 0.0), writes=[("d_Qz", i, 1)])
        lp = sb("d_lp", [128, 256], F32)
        prod = sb("d_prod", [128, 128], F32)
        s12 = sb("d_s12", [128, 2], F32)
        neglam = sb("d_neglam", [128, 1], F32)
        rs4 = sb("d_rs4", [128, 4], F32)
        t4 = sb("d_t4", [128, 4], F32)
        ss4 = sb("d_ss4", [128, 4], F32)
        c4 = sb("d_c4", [128, 4], F32)
        On0 = sb("d_On0", [128, 4, 128], F32)
        of = sb("d_of", [128, 4, 128], F32)
        sq = sb("d_sq", [128, 4, 128], F32)
        ob = [sb("d_ob%d" % i, [128, 4, 128], BF16) for i in range(2)]
        epsb = sb("d_epsb", [128, 1], F32)
        P.op("dve", lambda e: e.memset(epsb[:], EPS), writes=["d_epsb"])
        for i in range(2):
            P.op("pool", lambda e, i=i: e.memset(V[i][:, :, 128:129], 1.0), writes=[("d_V1", i)])
        P.dma(lambda e: e.dma_start(out=lp[:], in_=C.w["diff_lambda"][l:l + 1, :].partition_broadcast(128)), writes=["d_lp"])
        P.op("dve", lambda e: e.tensor_tensor(prod[:, 0:64], lp[:, 0:64], lp[:, 64:128], ALU.mult), reads=["d_lp"], writes=["d_prod0"])
        P.op("dve", lambda e: e.tensor_tensor(prod[:, 64:128], lp[:, 128:192], lp[:, 192:256], ALU.mult), reads=["d_lp"], writes=["d_prod1"])
        P.op("dve", lambda e: e.reduce_sum(s12[:, 0:1], prod[:, 0:64], axis=AX.X), reads=["d_prod0"], writes=["d_s0"])
        P.op("dve", lambda e: e.reduce_sum(s12[:, 1:2], prod[:, 64:128], axis=AX.X), reads=["d_prod1"], writes=["d_s1"])
        P.op("act", lambda e: e.activation(s12[:], s12[:], AF.Exp), reads=["d_s0", "d_s1"], writes=["d_e12"])
        P.op("dve", lambda e: e.tensor_tensor(neglam[:], s12[:, 1:2], s12[:, 0:1], ALU.subtract), reads=["d_e12"], writes=["d_neglam"])
        P.op("dve", lambda e: e.tensor_scalar(neglam[:], neglam[:], -lam_init, None, ALU.add), reads=["d_neglam"], writes=["d_neglam"])
        nob = 0

        def sums(par, dst):
            P.op("dve", lambda e: e.tensor_scalar(dst[:, 0:2], A.OTA[par][:, 128:258:129], 1e-20, None, ALU.max),
                 reads=[("a_OT", par)], writes=["d_rs4"])
            P.op("dve", lambda e: e.tensor_scalar(dst[:, 2:4], A.OTB[par][:, 128:258:129], 1e-20, None, ALU.max),
                 reads=[("a_OT", par)], writes=["d_rs4"])
            P.op("dve", lambda e: e.reciprocal(dst[:], dst[:]), reads=["d_rs4"], writes=["d_rs4"])

        for h in range(4):
            hb = h % 2
            P.dma(lambda e, h=h, hb=hb: e.dma_start(out=Q[hb][0][0:64, :], in_=C.QdT[h * 128:h * 128 + 64, :]), reads=["QdT_all"], writes=[("d_Q", hb, 0)])
            P.dma(lambda e, h=h, hb=hb: e.dma_start(out=Q[hb][1][64:128, :], in_=C.QdT[h * 128 + 64:(h + 1) * 128, :]), reads=["QdT_all"], writes=[("d_Q", hb, 1)])
            P.dma(lambda e, h=h, hb=hb: e.dma_start(out=Kt[hb][:], in_=C.KdT[h * 128:(h + 1) * 128, :]), reads=["KdT_all"], writes=[("d_K", hb)])
            P.dma(lambda e, h=h, hb=hb: e.dma_start(out=V[hb][:, :, 0:128], in_=C.Vd[:, h * 128:(h + 1) * 128].rearrange("(k p) c -> p k c", p=128)),
                  reads=["Vd_all", ("d_V1", hb)], writes=[("d_V", hb)])
            for qg in QORDER:
                q0 = qg * 512
                for m in range(2):
                    lo, hi = m * 64, (m + 1) * 64
                    steps = []
                    for kt in range(4 * qg + 4):
                        rel = kt - 4 * qg
                        mk = [(A.ident[:], A.masks[:, rel, :])] if rel >= 0 else []
                        steps.append((kt, mk, max(rel, 0) * 128, 512))
                    fin = None
                    if m == 1:
                        def fin(h=h, q0=q0, qg=qg):
                            nonlocal nob
                            ofk = [("d_of", qt) for qt in range(4)]
                            P.op("pool", lambda e: e.tensor_tensor(sq[:], of[:], of[:], ALU.mult), reads=ofk, writes=["d_sq"])
                            P.op("dve", lambda e: e.reduce_sum(ss4[:], sq[:], axis=AX.X), reads=["d_sq"], writes=["d_ss4"])
                            P.op("act", lambda e: e.activation(c4[:], ss4[:], AF.Ln, bias=epsb[:, 0:1], scale=1.0 / 128), reads=["d_ss4", "d_epsb"], writes=["d_c4"])
                            P.op("act", lambda e: e.activation(c4[:], c4[:], AF.Exp, scale=-0.5), reads=["d_c4"], writes=["d_c4"])
                            oi = nob % 2
                            nob += 1
                            for qt in range(4):
                                P.op("dve", lambda e, qt=qt, oi=oi: e.tensor_scalar(ob[oi][:, qt, :], of[:, qt, :], c4[:, qt:qt + 1], 1.0 - lam_init, ALU.mult, ALU.mult),
                                     reads=[("d_of", qt), "d_c4"], writes=[("d_ob", oi, qt)])
                            P.dma(lambda e, oi=oi: e.dma_start(
                                out=C.oatok[q0:q0 + 512, h * 128:(h + 1) * 128].rearrange("(t p) c -> p t c", p=128), in_=ob[oi][:]),
                                reads=[("d_ob", oi, qt) for qt in range(4)], writes=[("oatok", h, qg)])
                    def epi(par, m=m, fin=fin):
                        sums(par, rs4)
                        if m == 1:
                            P.op("dve", lambda e: e.tensor_scalar(t4[:], rs4[:], neglam[:, 0:1], None, ALU.mult),
                                 reads=["d_rs4", "d_neglam"], writes=["d_t4"])
                        for qt in range(4):
                            _, bank, off = ot_ap(A, par, qt, True)
                            if m == 0:
                                P.op("dve", lambda e, qt=qt, bank=bank, off=off: e.tensor_scalar(On0[:, qt, :], bank[:, off:off + 128], rs4[:, qt:qt + 1], None, ALU.mult),
                                     reads=[("a_OT", par), "d_rs4"], writes=[("d_On0", qt)])
                            else:
                                P.op("dve", lambda e, qt=qt, bank=bank, off=off: e.scalar_tensor_tensor(of[:, qt, :], bank[:, off:off + 128], t4[:, qt:qt + 1], On0[:, qt, :], ALU.mult, ALU.add),
                                     reads=[("a_OT", par), "d_t4", ("d_On0", qt)], writes=[("d_of", qt)])
                        if m == 1:
                            fin()
                    attn_call2(C, A, Q[hb][m][:, q0:q0 + 512],
                               lambda kt, hb=hb: Kt[hb][:, kt * 128:(kt + 1) * 128],
                               lambda kt, hb=hb: V[hb][:, kt, :], 129, steps,
                               [("d_Q", hb, m), ("d_Qz", hb, m), ("d_K", hb), ("d_V", hb), ("d_V1", hb)], True, epi)
            emit_calls(C, A)


def phase_nsa(C, l):
    P, nc = C.P, C.nc
    CMP_MASK = {31: 8, -481: 9, -993: 10, -1505: 11, -2017: 12}
    with ExitStack() as st:
        sb, ps = _alloc(C, st)
        A = attn_setup2(C, sb, ps, "n_")
        G = ps("n_G", [128, 512], F32)
        impAB = sb("n_impAB", [128, 2, 32, 64], F32)
        identf = sb("n_identf", [128, 128], F32)
        gtok = sb("n_gtok", [128, 32, 24], F32)
        QS = [sb("n_QS%d" % i, [128, 4, 512], BF16) for i in range(2)]
        ks = sb("n_ks", [128, T], BF16)
        kw = sb("n_kw", [128, T], BF16)
        vsp = sb("n_vsp", [128, 32, 65], BF16)
        vwp = sb("n_vwp", [128, 32, 65], BF16)
        kc = sb("n_kc", [128, 256], BF16)
        ovvc = sb("n_ovvc", [128, 2, 129], BF16)
        for i in range(2):
            P.op("pool", lambda e, i=i: e.memset(QS[i][64:128, :, :], 0.0), writes=[("n_QSs", i, hh) for hh in range(4)])
        P.op("pool", lambda e: e.memset(kw[64:128, :], 0.0), writes=["n_kwz"])
        P.op("pool", lambda e: e.memset(kc[64:128, :], 0.0), writes=["n_kcz"])
        rst = sb("n_rst", [128, 4], F32)
        rs4 = sb("n_rs4", [128, 4], F32)
        w4 = sb("n_w4", [128, 4], F32)
        impacc = sb("n_impacc", [128, 4, 64], F32)
        impf = sb("n_impf", [128, 4, 64], F32)
        top8 = sb("n_top8", [128, 4, 8], F32)
        selm = sb("n_selm", [128, 4, 128], F32)
        acc = [[sb("n_acc%d_%d" % (j, i), [128, 4, 64], F32) for i in range(4)] for j in range(2)]
        ob = [sb("n_ob%d" % i, [128, 4, 64], BF16) for i in range(2)]
        P.op("dve", lambda e: e.memset(selm[:], 0.0), writes=[("n_selm", qt) for qt in range(4)])
        P.op("pool", lambda e: e.memset(vsp[:, :, 64:65], 1.0), writes=["n_vs1"])
        P.op("pool", lambda e: e.memset(vwp[:, :, 64:65], 1.0), writes=["n_vw1"])
        P.op("pool", lambda e: e.memset(ovvc[:, :, 128:129], 1.0), writes=["n_ov1c"])
        P.dma(lambda e: e.dma_start(out=ovvc[:, :, 0:64], in_=C.c["ov1"][:, :, 0:64]), reads=["n_ov1c"], writes=["n_ov"])
        P.dma(lambda e: e.dma_start(out=ks[64:128, :], in_=C.c["E"]), writes=["n_E"])
        load_const(C, impAB[:], "impAB", "n_impAB")
        load_const(C, identf[:], "ident_f", "n_identf")
        P.dma(lambda e: e.dma_start(out=gtok[:], in_=C.gntok.rearrange("(t p) c -> p t c", p=128)), reads=["gntok_all"], writes=["n_gtok"])
        nob = 0
        nq = 0

        def epilogue(hh, h, par, branch, qg, ab):
            r = h * 3 + branch
            P.op("dve", lambda e: e.tensor_scalar(rs4[:], A.OTA[par][:, 64:512:128], 1e-20, None, ALU.max), reads=[("a_OT", par)], writes=["n_rs4"])
            P.op("dve", lambda e: e.reciprocal(rs4[:], rs4[:]), reads=["n_rs4"], writes=["n_rs4"])
            P.op("dve", lambda e: e.tensor_tensor(w4[:], rs4[:], gtok[:, qg * 4:(qg + 1) * 4, r], ALU.mult), reads=["n_rs4", "n_gtok"], writes=["n_w4"])
            for qt in range(4):
                P.op("dve", lambda e, qt=qt: e.scalar_tensor_tensor(acc[ab][hh][:, qt, :], A.OTA[par][:, qt * 128:qt * 128 + 64], w4[:, qt:qt + 1],
                                                                     acc[ab][hh][:, qt, :], ALU.mult, ALU.add),
                     reads=[("a_OT", par), "n_w4", ("n_acc", ab, hh, qt)], writes=[("n_acc", ab, hh, qt)])

        for g in range(2):
            lo, hi = g * 64, (g + 1) * 64
            P.dma(lambda e, lo=lo, hi=hi: e.dma_start(out=ks[0:64, :], in_=C.ksT[lo:hi, :]), reads=["ksT_all"], writes=["n_ks"])
            P.dma(lambda e, lo=lo, hi=hi: e.dma_start(out=kw[0:64, :], in_=C.kwT[lo:hi, :]), reads=["kwT_all"], writes=["n_kw"])
            P.dma(lambda e, lo=lo, hi=hi: e.dma_start(out=vsp[:, :, 0:64], in_=C.vs[:, lo:hi].rearrange("(k p) c -> p k c", p=128)),
                  reads=["vs_all", "n_vs1"], writes=["n_vs"])
            P.dma(lambda e, lo=lo, hi=hi: e.dma_start(out=vwp[:, :, 0:64], in_=C.vw[:, lo:hi].rearrange("(k p) c -> p k c", p=128)),
                  reads=["vw_all", "n_vw1"], writes=["n_vw"])
            P.dma(lambda e, lo=lo, hi=hi: e.dma_start(out=kc[0:64, :], in_=C.kcT[lo:hi, :]), reads=["kcT_all"], writes=["n_kc"])
            P.dma(lambda e, g=g: e.dma_start(out=ovvc[:, :, 64:128], in_=C.vc[g].rearrange("(c p) o -> p c o", p=128)), reads=["vc_all", "n_ov1c", "n_ov"], writes=["n_vc"])
            def stage_a(qg, qb, ab, g=g):
                q0 = qg * 512
                P.dma(lambda e, g=g, q0=q0, qb=qb: e.dma_start(
                    out=QS[qb][0:64, :, :], in_=C.QnT[g * 256:(g + 1) * 256, q0:q0 + 512].rearrange("(h d) t -> d h t", d=64)),
                    reads=["QnT_all"], writes=[("n_QSq", qb)])
                csteps = []
                th0 = 31 - 512 * qg
                csteps.append((0, [(A.ident[:], A.masks[:, CMP_MASK[th0], :])] if th0 in CMP_MASK else [], 0, 512))
                if qg >= 4:
                    th1 = 2079 - 512 * qg
                    csteps.append((1, [(A.ident[:], A.masks[:, CMP_MASK[th1], :])], 0, 512))
                for hh in range(4):
                    h = 4 * g + hh
                    def epi_c(par, hh=hh, h=h, qg=qg, ab=ab):
                        P.op("dve", lambda e: e.tensor_scalar(rst[:, 0:2], A.OTA[par][:, 128:258:129], 1e-20, None, ALU.max),
                             reads=[("a_OT", par)], writes=["n_rst"])
                        P.op("dve", lambda e: e.tensor_scalar(rst[:, 2:4], A.OTB[par][:, 128:258:129], 1e-20, None, ALU.max),
                             reads=[("a_OT", par)], writes=["n_rst"])
                        P.op("dve", lambda e: e.reciprocal(rst[:], rst[:]), reads=["n_rst"], writes=["n_rst"])
                        P.op("dve", lambda e: e.tensor_tensor(w4[:], rst[:], gtok[:, qg * 4:(qg + 1) * 4, h * 3], ALU.mult),
                             reads=["n_rst", "n_gtok"], writes=["n_w4"])
                        for qt in range(4):
                            _, bank, off = ot_ap(A, par, qt, True)
                            if hh == 0:
                                P.op("dve", lambda e, qt=qt, bank=bank, off=off: e.tensor_scalar(impacc[:, qt, :], bank[:, off:off + 64], rst[:, qt:qt + 1], None, ALU.mult),
                                     reads=[("a_OT", par), "n_rst"], writes=[("n_impacc", qt)])
                            else:
                                P.op("dve", lambda e, qt=qt, bank=bank, off=off: e.scalar_tensor_tensor(impacc[:, qt, :], bank[:, off:off + 64], rst[:, qt:qt + 1],
                                                                                                         impacc[:, qt, :], ALU.mult, ALU.add),
                                     reads=[("a_OT", par), "n_rst", ("n_impacc", qt)], writes=[("n_impacc", qt)])
                            P.op("dve", lambda e, qt=qt, bank=bank, off=off: e.tensor_scalar(acc[ab][hh][:, qt, :], bank[:, off + 64:off + 128], w4[:, qt:qt + 1], None, ALU.mult),
                                 reads=[("a_OT", par), "n_w4"], writes=[("n_acc", ab, hh, qt)])
                    attn_call2(C, A, QS[qb][:, hh, :], lambda ct: kc[:, ct * 128:(ct + 1) * 128],
                               lambda ct: ovvc[:, ct, :], 129, csteps,
                               [("n_QSq", qb), "n_kc", "n_kcz", "n_vc", "n_ov", "n_ov1c", ("n_QSs", qb, hh)], True, epi_c)
                emit_calls(C, A)
                ik = [("n_impacc", qt) for qt in range(4)]
                P.op("dve", lambda e, qg=qg: e.tensor_tensor(impf[:], impacc[:], impAB[:, 0, qg * 4:(qg + 1) * 4, :], ALU.mult),
                     reads=ik + ["n_impAB"], writes=["n_impf"])
                P.op("dve", lambda e, qg=qg: e.tensor_tensor(impf[:], impf[:], impAB[:, 1, qg * 4:(qg + 1) * 4, :], ALU.add),
                     reads=["n_impf", "n_impAB"], writes=["n_impf"])
                for qt in range(4):
                    P.op("dve", lambda e, qt=qt: e.max(top8[:, qt, :], impf[:, qt, :]), reads=["n_impf"], writes=[("n_top8", qt)])
                for qt in range(4):
                    P.op("dve", lambda e, qt=qt: e.tensor_scalar(selm[:, qt, 64:128], impf[:, qt, :], top8[:, qt, 7:8], 1.0, ALU.is_ge, ALU.subtract),
                         reads=["n_impf", ("n_top8", qt)], writes=[("n_selm", qt)])

                def ftr(e):
                    ins = None
                    for qt in range(4):
                        ins = e.transpose(G[:, qt * 128:(qt + 1) * 128], selm[:, qt, :], identf[:])
                    return ins
                P.op("pe", ftr, reads=[("n_selm", qt) for qt in range(4)] + ["n_identf"], writes=["n_G"])
                for hh in range(4):
                    P.op("act", lambda e, qb=qb, hh=hh: e.activation(QS[qb][64:128, hh, :], G[64:128, :], AF.Copy, scale=-NEG),
                         reads=["n_G"], writes=[("n_QSs", qb, hh)])

            def stage_b(qg, qb, ab, g=g):
                q0 = qg * 512
                for hh in range(4):
                    h = 4 * g + hh
                    ssteps = []
                    for kt in range(4 * qg + 4):
                        rel = kt - 4 * qg
                        mk = []
                        if rel >= 0:
                            mk.append((A.ident[:], A.masks[:, rel, :]))
                        ssteps.append((kt, mk, max(rel, 0) * 128, 512))
                    attn_call2(C, A, QS[qb][:, hh, :], lambda kt: ks[:, kt * 128:(kt + 1) * 128],
                               lambda kt: vsp[:, kt, :], 65, ssteps,
                               [("n_QSq", qb), "n_ks", "n_vs", "n_vs1", "n_E", ("n_QSs", qb, hh)], False,
                               lambda par, hh=hh, h=h, qg=qg, ab=ab: epilogue(hh, h, par, 1, qg, ab))
                    wsteps = []
                    for kt in range(max(0, 4 * qg - 4), 4 * qg + 4):
                        rel = kt - 4 * qg
                        mi = rel if rel >= 0 else 8 + rel
                        wc0 = max(rel, 0) * 128
                        wc1 = 512 if rel >= -1 else 512 + (rel + 1) * 128
                        wsteps.append((kt, [(A.ident[:], A.masks[:, mi, :])], wc0, wc1))

                    def epi_w(par, hh=hh, h=h, qg=qg, q0=q0, ab=ab):
                        nonlocal nob
                        epilogue(hh, h, par, 2, qg, ab)
                        oi = nob % 2
                        nob += 1
                        P.op("act", lambda e: e.copy(ob[oi][:], acc[ab][hh][:]), reads=[("n_acc", ab, hh, qt) for qt in range(4)], writes=[("n_ob", oi)])
                        P.dma(lambda e: e.dma_start(
                            out=C.obtok[q0:q0 + 512, h * 64:(h + 1) * 64].rearrange("(t p) c -> p t c", p=128), in_=ob[oi][:]),
                            reads=[("n_ob", oi)], writes=[("obtok", h, qg)])
                    attn_call2(C, A, QS[qb][:, hh, :], lambda kt: kw[:, kt * 128:(kt + 1) * 128],
                               lambda kt: vwp[:, kt, :], 65, wsteps,
                               [("n_QSq", qb), ("n_QSs", qb, hh), "n_kw", "n_kwz", "n_vw", "n_vw1"], False, epi_w)
                emit_calls(C, A)

            stage_a(QORDER[0], 0, 0)
            for i in range(NG):
                if i + 1 < NG:
                    stage_a(QORDER[i + 1], (i + 1) % 2, (i + 1) % 2)
                stage_b(QORDER[i], i % 2, i % 2)


def phase_outp(C, l):
    P, nc = C.P, C.nc
    with ExitStack() as st:
        sb, ps = _alloc(C, st)
        Wa = sb("o_Wa", [128, 4, D], BF16)
        Wb = sb("o_Wb", [128, 4, D], BF16)
        Wo = sb("o_Wo", [128, 8, D], BF16)
        oa = sb("o_oa", [128, 4, 512], BF16)
        obx = sb("o_ob", [128, 4, 512], BF16)
        ga = sb("o_ga", [128, 8, 512], F32)
        gb = sb("o_gb", [128, 8, 512], F32)
        hin = sb("o_hin", [128, 4, D], F32)
        t1 = [sb("o_t1%d" % i, [128, 512], F32) for i in range(2)]
        t2 = [sb("o_t2%d" % i, [128, 512], F32) for i in range(2)]
        yT = sb("o_yT", [128, 8, 512], BF16)
        hout = [sb("o_hout%d" % i, [128, D], F32) for i in range(2)]
        oat = sb("o_oat", [128, 4, 512], BF16)
        obt = sb("o_obt", [128, 4, 512], BF16)
        identb = sb("o_identb", [128, 128], BF16)
        load_const(C, identb[:], "ident_bf", "o_identb")
        ptr = [ps("o_ptr%d" % i, [128, 4, 128], BF16) for i in range(2)]
        pa = [ps("o_pa%d" % i, [128, 512], F32) for i in range(2)]
        pb = [ps("o_pb%d" % i, [128, 512], F32) for i in range(2)]
        po = [ps("o_po%d" % i, [128, 512], F32) for i in range(2)]
        P.dma(lambda e: e.dma_start(out=Wa[:], in_=C.w["w_branch_a"][l].rearrange("(c p) f -> p c f", p=128)), writes=["o_Wa"], q="pool")
        P.dma(lambda e: e.dma_start(out=Wb[:], in_=C.w["w_branch_b"][l].rearrange("(c p) f -> p c f", p=128)), writes=["o_Wb"], q="pool")
        P.dma(lambda e: e.dma_start(out=Wo[:], in_=C.w["w_out"][l].rearrange("(c p) f -> p c f", p=128)), writes=["o_Wo"], q="pool")
        for g in range(NG):
            r0 = g * 512
            P.dma(lambda e, r0=r0: e.dma_start(out=oat[:], in_=C.oatok[r0:r0 + 512, :].rearrange("(t p) c -> p t c", p=128)),
                  reads=["oatok_all"], writes=["o_oat"])
            P.dma(lambda e, r0=r0: e.dma_start(out=obt[:], in_=C.obtok[r0:r0 + 512, :].rearrange("(t p) c -> p t c", p=128)),
                  reads=["obtok_all"], writes=["o_obt"])
            ntr = 0
            for srct, dstt, sk, dk in ((oat, oa, "o_oat", "o_oa"), (obt, obx, "o_obt", "o_ob")):
                for tt in range(4):
                    pb_ = ntr % 2
                    ntr += 1

                    def ftr(e, srct=srct, tt=tt, pb_=pb_):
                        ins = None
                        for kc in range(4):
                            ins = e.transpose(ptr[pb_][:, kc, :], srct[:, tt, kc * 128:(kc + 1) * 128], identb[:])
                        return ins
                    P.op("pe", ftr, reads=[sk, "o_identb"], writes=[("o_ptr", pb_)])
                    P.op("act", lambda e, dstt=dstt, tt=tt, pb_=pb_: e.copy(dstt[:, :, tt * 128:(tt + 1) * 128], ptr[pb_][:, :, :]),
                         reads=[("o_ptr", pb_)], writes=[(dk, tt)])
            P.dma(lambda e, r0=r0: e.dma_start(out=ga[:], in_=C.gaT[:, r0:r0 + 512].rearrange("(c p) t -> p c t", p=128)),
                  reads=["gaT_all"], writes=["o_ga"])
            P.dma(lambda e, r0=r0: e.dma_start(out=gb[:], in_=C.gbT[:, r0:r0 + 512].rearrange("(c p) t -> p c t", p=128)),
                  reads=["gbT_all"], writes=["o_gb"])
            P.dma(lambda e, r0=r0: e.dma_start(out=hin[:], in_=C.hbuf[r0:r0 + 512, :].rearrange("(t p) d -> p t d", p=128)),
                  reads=[("hdram", g)], writes=[("o_hin", tt) for tt in range(4)])
            for fb in range(8):
                b = fb % 2

                def mm(e, W, X, dstp, fb=fb):
                    ins = None
                    for kc in range(4):
                        ins = e.matmul(dstp[:], W[:, kc, fb * 128:(fb + 1) * 128], X[:, kc, :], start=(kc == 0), stop=(kc == 3))
                    return ins
                P.op("pe", lambda e, b=b, mm=mm: mm(e, Wa, oa, pa[b]), reads=["o_Wa"] + [("o_oa", tt) for tt in range(4)], writes=[("o_pa", b)])
                P.op("pe", lambda e, b=b, mm=mm: mm(e, Wb, obx, pb[b]), reads=["o_Wb"] + [("o_ob", tt) for tt in range(4)], writes=[("o_pb", b)])
                P.op("dve", lambda e, b=b, fb=fb: e.tensor_tensor(t1[b][:], pa[b][:], ga[:, fb, :], ALU.mult),
                     reads=[("o_pa", b), "o_ga"], writes=[("o_t1", b)])
                P.op("dve", lambda e, b=b, fb=fb: e.tensor_tensor(t2[b][:], pb[b][:], gb[:, fb, :], ALU.mult),
                     reads=[("o_pb", b), "o_gb"], writes=[("o_t2", b)])
                P.op("pool", lambda e, b=b, fb=fb: e.tensor_tensor(yT[:, fb, :], t1[b][:], t2[b][:], ALU.add),
                     reads=[("o_t1", b), ("o_t2", b)], writes=[("o_yT", fb)])
            yk = [("o_yT", fb) for fb in range(8)]
            for tt in range(4):
                hb = tt % 2
                for half in range(2):
                    b = half

                    def mmo(e, tt=tt, half=half, b=b):
                        ins = None
                        for fb in range(8):
                            ins = e.matmul(po[b][:], yT[:, fb, tt * 128:(tt + 1) * 128], Wo[:, fb, half * 512:(half + 1) * 512],
                                           start=(fb == 0), stop=(fb == 7))
                        return ins
                    P.op("pe", mmo, reads=yk + ["o_Wo"], writes=[("o_po", b)])
                    P.op("dve", lambda e, tt=tt, half=half, b=b, hb=hb: e.tensor_tensor(
                        hout[hb][:, half * 512:(half + 1) * 512], po[b][:], hin[:, tt, half * 512:(half + 1) * 512], ALU.add),
                        reads=[("o_po", b), ("o_hin", tt)], writes=[("o_hout", hb, half)])
                rr = r0 + tt * 128
                P.dma(lambda e, rr=rr, hb=hb: e.dma_start(out=C.hbuf[rr:rr + 128, :], in_=hout[hb][:]),
                      reads=[("o_hout", hb, 0), ("o_hout", hb, 1)], writes=[("hdram", g)])


def prep_inputs(inputs):
    bf = ml_dtypes.bfloat16
    w = {}
    for n, s in WEIGHT_SPECS:
        if n == "w_in_perm":
            continue
        w[n] = np.ascontiguousarray(np.asarray(inputs[n], dtype=np.float32).reshape(s))
    idx = []
    for c0, n in ROPE_COLS:
        for h0 in range(c0, c0 + n, 64):
            idx += list(range(h0 + 8, h0 + 16)) + list(range(h0, h0 + 8)) + list(range(h0 + 16, h0 + 64))
    w["w_in_perm"] = np.ascontiguousarray(w["w_in"][:, :, np.asarray(idx)])
    consts = make_consts()
    x = np.asarray(inputs["x"], dtype=np.float32)
    pos = np.asarray(inputs["positions"]).astype(np.int32)
    in_maps = []
    for b in range(8):
        m = {"x": np.ascontiguousarray(x[b]), "pos": np.ascontiguousarray(pos[b:b + 1])}
        m.update(w)
        m.update(consts)
        in_maps.append(m)
    return in_maps


_CACHE = {}


def kernel(**inputs):
    in_maps = prep_inputs(inputs)
    if "nc" not in _CACHE:
        _CACHE["nc"] = build("all")[0]
    nc = _CACHE["nc"]
    res = run_bass_kernel_spmd(nc, in_maps, core_ids=list(range(8)))
    out = np.stack([np.asarray(r["out"], dtype=np.float32).reshape(T, D) for r in res.results], 0)
    return out
```

```python
import math
import numpy as np
import ml_dtypes
from contextlib import ExitStack
import concourse.bass as bass
import concourse.mybir as mybir
from concourse.bass_utils import run_bass_kernel_spmd

F32 = mybir.dt.float32
BF16 = mybir.dt.bfloat16
I32 = mybir.dt.int32
AF = mybir.ActivationFunctionType
ALU = mybir.AluOpType
AX = mybir.AxisListType

T = 4096
D = 1024
DFF = 2816
NFC = DFF // 128
DEPTH = 2
IN_COLS = 4888
EPS = 1e-6
NG = T // 512
NEG = -30000.0
ROPE_COLS = [(0, 512), (512, 512), (1536, 512), (2304, 128), (2560, 128)]
NPERM = 1792
QORDER = [7, 0, 6, 1, 5, 2, 4, 3]

COMPUTE = ("pe", "act", "dve", "pool")
NDMA_SEMS = 48
NSW_SEMS = 12


class Prog:
    def __init__(self, nc, stack):
        self.nc = nc
        self.ops = {e: [] for e in ("pe", "act", "dve", "pool", "sp")}
        self.cnt = {e: 0 for e in COMPUTE}
        self.sems = {}
        for e in COMPUTE:
            self.sems["e_" + e] = stack.enter_context(nc.semaphore("s_" + e))
        for i in range(NDMA_SEMS):
            self.sems["d%d" % i] = stack.enter_context(nc.semaphore("s_d%d" % i))
        self.dval = [0] * NDMA_SEMS
        self.dnext = 0
        self.dnext_sw = 0
        self.last_write = {}
        self.readers = {}
        self.waited = {e: {} for e in self.ops}
        self.nwaits = 0
        self.nops = 0

    def _deps(self, eng, reads, writes):
        need = {}

        def add(tok):
            if tok is None:
                return
            s, v = tok
            if need.get(s, 0) < v:
                need[s] = v
        for k in reads:
            add(self.last_write.get(k))
        for k in writes:
            add(self.last_write.get(k))
            for s, v in self.readers.get(k, {}).items():
                add((s, v))
        out = []
        for s, v in need.items():
            if eng == "pe" and s == "e_pe":
                continue
            if self.waited[eng].get(s, 0) >= v:
                continue
            self.waited[eng][s] = v
            out.append((s, v))
        return out

    def _commit(self, tok, reads, writes):
        s, v = tok
        for k in reads:
            r = self.readers.setdefault(k, {})
            if r.get(s, 0) < v:
                r[s] = v
        for k in writes:
            self.last_write[k] = tok
            self.readers[k] = {}

    def op(self, eng, fn, reads=(), writes=()):
        waits = self._deps(eng, reads, writes)
        self.cnt[eng] += 1
        tok = ("e_" + eng, self.cnt[eng])
        self.ops[eng].append((waits, fn, ("e_" + eng, 1)))
        self._commit(tok, reads, writes)
        self.nwaits += len(waits)
        self.nops += 1

    def dma(self, fn, reads=(), writes=(), q="sp"):
        if q == "pool":
            i = self.dnext_sw
            self.dnext_sw = (self.dnext_sw + 1) % NSW_SEMS
        else:
            i = NSW_SEMS + self.dnext
            self.dnext = (self.dnext + 1) % (NDMA_SEMS - NSW_SEMS)
        s = "d%d" % i
        waits = self._deps(q, reads, writes)
        if self.dval[i] > 0 and self.waited[q].get(s, 0) < self.dval[i]:
            self.waited[q][s] = self.dval[i]
            waits.append((s, self.dval[i]))
        self.dval[i] += 16
        tok = (s, self.dval[i])
        self.ops[q].append((waits, fn, (s, 16)))
        self._commit(tok, reads, writes)
        self.nwaits += len(waits)
        self.nops += 1

    def barrier(self):
        toks = []
        for i in range(NDMA_SEMS):
            if self.dval[i] > 0:
                toks.append(("d%d" % i, self.dval[i]))
        for e in COMPUTE:
            if self.cnt[e] > 0:
                toks.append(("e_" + e, self.cnt[e]))
        for e in self.ops:
            waits = []
            for s, v in toks:
                if e == "pe" and s == "e_pe":
                    continue
                if self.waited[e].get(s, 0) >= v:
                    continue
                self.waited[e][s] = v
                waits.append((s, v))
            if waits:
                self.ops[e].append((waits, None, None))
                self.nwaits += len(waits)
        self.last_write = {}
        self.readers = {}

    def emit(self):
        nc = self.nc
        P = self

        def replay(name, eng):
            for waits, fn, inc in P.ops[name]:
                for s, v in waits:
                    eng.wait_ge(P.sems[s], v)
                if fn is None:
                    continue
                ins = fn(eng)
                ins.then_inc(P.sems[inc[0]], inc[1])

        with nc.Block() as block:
            @block.tensor
            def _(eng):
                replay("pe", eng)

            @block.scalar
            def _(eng):
                replay("act", eng)

            @block.vector
            def _(eng):
                replay("dve", eng)

            @block.gpsimd
            def _(eng):
                replay("pool", eng)

            @block.sync
            def _(eng):
                replay("sp", eng)


def make_consts():
    bf = ml_dtypes.bfloat16
    c = {}
    c["ident_bf"] = np.eye(128, dtype=np.float32).astype(bf)
    c["ident_f"] = np.eye(128, dtype=np.float32)
    c["ones_bf"] = np.ones((128, 128), np.float32).astype(bf)
    half = 8
    inv = 500000.0 ** (-2.0 * np.arange(half, dtype=np.float64) / 16.0)
    rp = np.zeros((128, 2), np.float32)
    for p in range(128):
        d = p % 64
        if d < 16:
            rp[p, 0] = inv[d % 8] / (2 * np.pi)
            rp[p, 1] = -1.0 if d < 8 else 1.0
    c["ropep"] = rp
    p = np.arange(128)[:, None]
    f = np.arange(512)[None, :]
    masks = []
    for rel in range(4):
        masks.append(np.where(f - p >= rel * 128, 0.0, NEG))
    for rel in (-4, -3, -2, -1):
        masks.append(np.where(f - p < 512 + rel * 128, 0.0, NEG))
    for th in (31, -481, -993, -1505, -2017):
        masks.append(np.where(f - 16 * p - th >= 0, 0.0, NEG))
    c["masks"] = np.stack(masks, 1).astype(np.float32).astype(bf)
    j = np.arange(64)[:, None]
    k = np.arange(T)[None, :]
    c["E"] = (k // 64 == j).astype(np.float32).astype(bf)
    cs = np.arange(256) * 16
    ss = np.arange(64) * 64
    ov = np.clip(np.minimum(cs[:, None] + 32, ss[None, :] + 64) - np.maximum(cs[:, None], ss[None, :]), 0, None) / 32.0
    ov[255, :] = 0.0
    ov1 = np.concatenate([ov, np.ones((256, 1))], 1).astype(np.float32)
    c["ov1"] = ov1.reshape(2, 128, 65).transpose(1, 0, 2).copy().astype(bf)
    q = np.arange(T)[:, None]
    jj = np.arange(64)[None, :]
    cur = q // 64
    forced = (jj == 0) | (jj == cur) | (jj == cur - 1)
    future = (jj * 64) > q
    A = np.where(future | forced, 0.0, 1.0)
    Bm = np.where(future, -1.0, np.where(forced, 1.0e4, 0.0))
    AB = np.stack([A, Bm], 0).astype(np.float32)
    c["impAB"] = AB.reshape(2, 32, 128, 64).transpose(2, 0, 1, 3).copy()
    sg = np.zeros((24, 24, 64), np.float32)
    for r in range(24):
        sg[r, r, :] = 1.0
    c["selg"] = sg
    return c


CONST_SPECS = [("ident_bf", [128, 128], BF16), ("ident_f", [128, 128], F32), ("ones_bf", [128, 128], BF16),
               ("ropep", [128, 2], F32), ("masks", [128, 13, 512], BF16), ("E", [64, T], BF16),
               ("ov1", [128, 2, 65], BF16), ("impAB", [128, 2, 32, 64], F32), ("selg", [24, 24, 64], F32)]

WEIGHT_SPECS = [("ffn1_norm", [DEPTH, D]), ("ffn1_w_gu", [DEPTH, D, 2 * DFF]), ("ffn1_w_down", [DEPTH, DFF, D]),
                ("mix_norm", [DEPTH, D]), ("w_in", [DEPTH, D, IN_COLS]), ("w_in_perm", [DEPTH, D, NPERM]),
                ("diff_lambda", [DEPTH, 256]), ("cmp_pos", [DEPTH, 2, 32, 64]), ("cmp_w1", [DEPTH, 2, 2048, 256]),
                ("cmp_w2", [DEPTH, 2, 256, 64]), ("w_branch_a", [DEPTH, 512, D]), ("w_branch_b", [DEPTH, 512, D]),
                ("w_out", [DEPTH, D, D]), ("ffn2_norm", [DEPTH, D]), ("ffn2_w_gu", [DEPTH, D, 2 * DFF]),
                ("ffn2_w_down", [DEPTH, DFF, D]), ("final_norm", [1, D])]


class Ctx:
    pass


def build(upto="all", taps=()):
    nc = bass.Bass("TRN2", target_bir_lowering=False)
    C = Ctx()
    C.nc = nc
    di = lambda n, s, d: nc.dram_tensor(n, s, d, kind="ExternalInput").ap()
    C.x = di("x", [T, D], F32)
    C.pos = di("pos", [1, T], I32)
    C.w = {n: di(n, s, F32) for n, s in WEIGHT_SPECS}
    C.c = {n: di(n, s, d) for n, s, d in CONST_SPECS}
    C.out = nc.dram_tensor("out", [T, D], F32, kind="ExternalOutput").ap()

    def scr(n, s, d):
        kind = "ExternalOutput" if n in taps else "Internal"
        return nc.dram_tensor(n, s, d, kind=kind).ap()
    C.hbuf = scr("hbuf", [T, D], F32)
    C.ropeC = scr("ropeC", [128, T], F32)
    C.ropeS = scr("ropeS", [128, T], F32)
    C.QdT = scr("QdT", [512, T], BF16)
    C.KdT = scr("KdT", [512, T], BF16)
    C.Vd = scr("Vd", [T, 512], BF16)
    C.QnT = scr("QnT", [512, T], BF16)
    C.kcrT = scr("kcrT", [128, T], BF16)
    C.vcrT = scr("vcrT", [128, T], BF16)
    C.ksT = scr("ksT", [128, T], BF16)
    C.kwT = scr("kwT", [128, T], BF16)
    C.vs = scr("vs", [T, 128], BF16)
    C.vw = scr("vw", [T, 128], BF16)
    C.gntok = scr("gntok", [T, 24], F32)
    C.oatok = scr("oatok", [T, 512], BF16)
    C.obtok = scr("obtok", [T, 512], BF16)
    C.gaT = scr("gaT", [D, T], F32)
    C.gbT = scr("gbT", [D, T], F32)
    C.kcT = scr("kcT", [128, 256], BF16)
    C.vc = scr("vc", [2, 256, 64], BF16)
    C.oaT = scr("oaT", [512, T], BF16)
    C.obT = scr("obT", [512, T], BF16)

    order = ["rope"]
    for l in range(DEPTH):
        order += ["ffn1_%d" % l, "proj_%d" % l, "cmp_%d" % l, "diff_%d" % l, "nsa_%d" % l, "outp_%d" % l, "ffn2_%d" % l]
    if upto == "all":
        upto = order[-1]
    todo = order[:order.index(upto) + 1]

    with ExitStack() as st:
        P = Prog(nc, st)
        C.P = P
        for ph in todo:
            if ph == "rope":
                phase_rope(C)
            else:
                name, l = ph.split("_")
                l = int(l)
                if name == "ffn1":
                    phase_ffn(C, C.x if l == 0 else C.hbuf, C.hbuf, C.w["ffn1_w_gu"][l], C.w["ffn1_w_down"][l],
                              C.w["ffn1_norm"][l:l + 1, :], None)
                elif name == "proj":
                    phase_proj(C, l)
                elif name == "cmp":
                    phase_cmp(C, l)
                elif name == "diff":
                    phase_diff(C, l)
                elif name == "nsa":
                    phase_nsa(C, l)
                elif name == "outp":
                    phase_outp(C, l)
                elif name == "ffn2":
                    last = (l == DEPTH - 1)
                    phase_ffn(C, C.hbuf, C.hbuf, C.w["ffn2_w_gu"][l], C.w["ffn2_w_down"][l],
                              C.w["ffn2_norm"][l:l + 1, :], C.w["final_norm"][0:1, :] if last else None)
            P.barrier()
        P.emit()
    C.nops = P.nops
    C.nwaits = P.nwaits
    return nc, C


def _alloc(C, st):
    nc = C.nc
    C.pid = getattr(C, "pid", 0) + 1
    sfx = "_%d" % C.pid
    sb = lambda n, s, d: st.enter_context(nc.sbuf_tensor(n + sfx, s, d))
    ps = lambda n, s, d: st.enter_context(nc.psum_tensor(n + sfx, s, d))
    return sb, ps


def load_const(C, tile_ap, name, key):
    C.P.dma(lambda e: e.dma_start(out=tile_ap, in_=C.c[name]), writes=[key])


def norm_stats(C, pfx, hin, ss, rstd, junk, hkey=None):
    P = C.P
    hkey = hkey or (pfx + "hin")
    for tt in range(4):
        P.op("act", lambda e, tt=tt: e.activation(junk[:], hin[:, tt, :], AF.Square, accum_out=ss[:, tt:tt + 1]),
             reads=[(hkey, tt)], writes=[pfx + "junk", (pfx + "ss", tt)])
    P.op("dve", lambda e: e.tensor_scalar(rstd[:], ss[:], 1.0 / D, EPS, ALU.mult, ALU.add),
         reads=[(pfx + "ss", tt) for tt in range(4)], writes=[pfx + "rstd"])
    P.op("act", lambda e: e.sqrt(rstd[:], rstd[:]), reads=[pfx + "rstd"], writes=[pfx + "rstd"])
    P.op("dve", lambda e: e.reciprocal(rstd[:], rstd[:]), reads=[pfx + "rstd"], writes=[pfx + "rstd"])


def norm_tile(C, pfx, tt, hin, gain, ident, rstd, xn, pt, xnT, hkey=None, xkey=None):
    P = C.P
    b = tt % 2
    hkey = hkey or (pfx + "hin")
    xkey = xkey or (pfx + "xnT")
    P.op("dve", lambda e: e.scalar_tensor_tensor(xn[b][:], hin[:, tt, :], rstd[:, tt:tt + 1], gain[:], ALU.mult, ALU.mult),
         reads=[(hkey, tt), pfx + "rstd", pfx + "gain"], writes=[(pfx + "xn", b)])

    def tr(e):
        ins = None
        for kc in range(8):
            ins = e.transpose(pt[:, kc, :], xn[b][:, kc * 128:(kc + 1) * 128], ident[:])
        return ins
    P.op("pe", tr, reads=[(pfx + "xn", b), pfx + "ident"], writes=[pfx + "pt"])
    P.op("act", lambda e: e.copy(xnT[:, :, tt * 128:(tt + 1) * 128], pt[:, :, :]),
         reads=[pfx + "pt"], writes=[(xkey, tt)])


def norm_group(C, pfx, hin, gain, ident, ss, rstd, junk, xn, pt, xnT):
    norm_stats(C, pfx, hin, ss, rstd, junk)
    for tt in range(4):
        norm_tile(C, pfx, tt, hin, gain, ident, rstd, xn, pt, xnT)


def phase_ffn(C, src, dst, wgu_d, wd_d, gain_d, fgain_d):
    P, nc = C.P, C.nc
    with ExitStack() as st:
        sb, ps = _alloc(C, st)
        wgu = sb("f_wgu", [128, 8, 2 * DFF], BF16)
        wd = sb("f_wd", [128, NFC, D], BF16)
        gain = sb("f_gain", [128, D], F32)
        fgain = sb("f_fgain", [128, D], F32) if fgain_d is not None else None
        ident = sb("f_ident", [128, 128], BF16)
        hin = sb("f_hin", [128, 4, D], F32)
        junk = sb("f_junk", [128, D], BF16)
        xn = [sb("f_xn%d" % i, [128, D], BF16) for i in range(2)]
        xnT = sb("f_xnT", [128, 8, 512], BF16)
        hh = sb("f_hh", [128, NFC, 512], BF16)
        sg = [sb("f_sg%d" % i, [128, 512], BF16) for i in range(2)]
        hout = [sb("f_hout%d" % i, [128, D], F32) for i in range(2)]
        oute = [sb("f_oute0", [128, D], F32)] * 2 if fgain_d is not None else None
        ss = sb("f_ss", [128, 4], F32)
        rstd = sb("f_rstd", [128, 4], F32)
        ss2 = sb("f_ss2", [128, 2], F32)
        rstd2 = sb("f_rstd2", [128, 2], F32)
        pg = [ps("f_pg%d" % i, [128, 512], F32) for i in range(2)]
        pu = [ps("f_pu%d" % i, [128, 512], F32) for i in range(2)]
        po = [ps("f_po%d" % i, [128, 512], F32) for i in range(2)]
        pt = ps("f_pt", [128, 8, 128], BF16)

        load_const(C, ident[:], "ident_bf", "f_ident")
        P.dma(lambda e: e.dma_start(out=gain[:], in_=gain_d.partition_broadcast(128)), writes=["f_gain"])
        if fgain_d is not None:
            P.dma(lambda e: e.dma_start(out=fgain[:], in_=fgain_d.partition_broadcast(128)), writes=["f_fgain"])
        FCH = [(0, 6), (6, 12), (12, 17), (17, 22)]
        for ci, (fa, fb) in enumerate(FCH):
            for hf in range(2):
                ca, cb = hf * DFF + fa * 128, hf * DFF + fb * 128
                P.dma(lambda e, ca=ca, cb=cb: e.dma_start(out=wgu[:, :, ca:cb], in_=wgu_d[:, ca:cb].rearrange("(kc p) f -> p kc f", p=128)),
                      writes=[("f_wgu", hf, ci)], q="pool")
        fc2ci = {}
        for ci, (fa, fb) in enumerate(FCH):
            for fc in range(fa, fb):
                fc2ci[fc] = ci
        for i in range(2):
            P.dma(lambda e, i=i: e.dma_start(out=wd[:, i * 11:(i + 1) * 11, :],
                                             in_=wd_d[i * 1408:(i + 1) * 1408, :].rearrange("(c p) f -> p c f", p=128)),
                  writes=[("f_wd", i)], q="pool")
        wd_keys = [("f_wd", 0), ("f_wd", 1)]
        pipe = fgain_d is None
        res = [sb("f_res%d" % i, [128, D], F32) for i in range(2)] if pipe else None

        def load_stats(g):
            r0 = g * 512
            P.dma(lambda e: e.dma_start(out=hin[:], in_=src[r0:r0 + 512, :].rearrange("(t p) d -> p t d", p=128)),
                  reads=[("hdram", g, tt) for tt in range(4)], writes=[("f_hin", tt) for tt in range(4)])
            norm_stats(C, "f_", hin, ss, rstd, junk)

        for g in range(NG):
            r0 = g * 512
            if not pipe or g == 0:
                load_stats(g)
                for tt in range(4):
                    norm_tile(C, "f_", tt, hin, gain, ident, rstd, xn, pt, xnT)
            xk = [("f_xnT", tt) for tt in range(4)]
            for fc in range(NFC):
                b = fc % 2

                def mm(e, col0, dstp):
                    ins = None
                    for kc in range(8):
                        ins = e.matmul(dstp[:], wgu[:, kc, col0:col0 + 128], xnT[:, kc, :], start=(kc == 0), stop=(kc == 7))
                    return ins
                P.op("pe", lambda e, fc=fc, b=b: mm(e, fc * 128, pg[b]), reads=xk + [("f_wgu", 0, fc2ci[fc])], writes=[("f_pg", b)])
                P.op("pe", lambda e, fc=fc, b=b: mm(e, DFF + fc * 128, pu[b]), reads=xk + [("f_wgu", 1, fc2ci[fc])], writes=[("f_pu", b)])
                P.op("act", lambda e, b=b: e.activation(sg[b][:], pg[b][:], AF.Silu), reads=[("f_pg", b)], writes=[("f_sg", b)])
                P.op("dve", lambda e, fc=fc, b=b: e.tensor_tensor(hh[:, fc, :], sg[b][:], pu[b][:], ALU.mult),
                     reads=[("f_sg", b), ("f_pu", b)], writes=[("f_hh", fc)])
            hk = [("f_hh", fc) for fc in range(NFC)]
            if pipe and g + 1 < NG:
                load_stats(g + 1)
            for tt in range(4):
                hb = tt % 2
                if pipe:
                    P.dma(lambda e, tt=tt, hb=hb, r0=r0: e.dma_start(out=res[hb][:], in_=src[r0 + tt * 128:r0 + (tt + 1) * 128, :]),
                          reads=[("hdram", g, tt)], writes=[("f_res", hb)])
                for half in range(2):
                    b = half

                    def mmd(e, tt=tt, half=half, b=b):
                        ins = None
                        for fc in range(NFC):
                            ins = e.matmul(po[b][:], hh[:, fc, tt * 128:(tt + 1) * 128], wd[:, fc, half * 512:(half + 1) * 512],
                                           start=(fc == 0), stop=(fc == NFC - 1))
                        return ins
                    P.op("pe", mmd, reads=hk + wd_keys, writes=[("f_po", b)])
                    if pipe:
                        P.op("dve", lambda e, half=half, b=b, hb=hb: e.scalar_tensor_tensor(
                            hout[hb][:, half * 512:(half + 1) * 512], po[b][:], 0.5, res[hb][:, half * 512:(half + 1) * 512],
                            ALU.mult, ALU.add), reads=[("f_po", b), ("f_res", hb)], writes=[("f_hout", hb, half)])
                    else:
                        P.op("dve", lambda e, tt=tt, half=half, b=b, hb=hb: e.scalar_tensor_tensor(
                            hout[hb][:, half * 512:(half + 1) * 512], po[b][:], 0.5, hin[:, tt, half * 512:(half + 1) * 512],
                            ALU.mult, ALU.add), reads=[("f_po", b), ("f_hin", tt)], writes=[("f_hout", hb, half)])
                hkeys = [("f_hout", hb, 0), ("f_hout", hb, 1)]
                rr = r0 + tt * 128
                if fgain_d is None:
                    P.dma(lambda e, rr=rr, hb=hb: e.dma_start(out=dst[rr:rr + 128, :], in_=hout[hb][:]),
                          reads=hkeys, writes=[("hdram", g, tt)])
                    if pipe and g + 1 < NG:
                        norm_tile(C, "f_", tt, hin, gain, ident, rstd, xn, pt, xnT)
                else:
                    P.op("act", lambda e, hb=hb: e.activation(junk[:], hout[hb][:], AF.Square, accum_out=ss2[:, hb:hb + 1]),
                         reads=hkeys, writes=["f_junk", ("f_ss2", hb)])
                    P.op("dve", lambda e, hb=hb: e.tensor_scalar(rstd2[:, hb:hb + 1], ss2[:, hb:hb + 1], 1.0 / D, EPS, ALU.mult, ALU.add),
                         reads=[("f_ss2", hb)], writes=[("f_rstd2", hb)])
                    P.op("act", lambda e, hb=hb: e.sqrt(rstd2[:, hb:hb + 1], rstd2[:, hb:hb + 1]),
                         reads=[("f_rstd2", hb)], writes=[("f_rstd2", hb)])
                    P.op("dve", lambda e, hb=hb: e.reciprocal(rstd2[:, hb:hb + 1], rstd2[:, hb:hb + 1]),
                         reads=[("f_rstd2", hb)], writes=[("f_rstd2", hb)])
                    P.op("dve", lambda e, hb=hb: e.scalar_tensor_tensor(oute[hb][:], hout[hb][:], rstd2[:, hb:hb + 1], fgain[:],
                                                                        ALU.mult, ALU.mult),
                         reads=hkeys + [("f_rstd2", hb), "f_fgain"], writes=["f_oute"])
                    P.dma(lambda e, rr=rr, hb=hb: e.dma_start(out=C.out[rr:rr + 128, :], in_=oute[hb][:]),
                          reads=["f_oute"], writes=[("outdram", g, tt)])


def phase_rope(C):
    P, nc = C.P, C.nc
    with ExitStack() as st:
        sb, ps = _alloc(C, st)
        posi = sb("r_posi", [128, T], I32)
        r = sb("r_r", [128, T], F32)
        ki = sb("r_ki", [128, T], I32)
        kf = sb("r_kf", [128, T], F32)
        f = sb("r_f", [128, T], F32)
        m = sb("r_m", [128, T], F32)
        rp = sb("r_rp", [128, 2], F32)
        load_const(C, rp[:], "ropep", "r_rp")
        P.dma(lambda e: e.dma_start(out=posi[:], in_=C.pos.partition_broadcast(128)), writes=["r_posi"])
        for which in range(2):
            P.op("dve", lambda e: e.tensor_copy(r[:], posi[:]), reads=["r_posi"], writes=["r_r"])
            if which == 0:
                P.op("dve", lambda e: e.tensor_scalar(r[:], r[:], rp[:, 0:1], None, ALU.mult), reads=["r_r", "r_rp"], writes=["r_r"])
            else:
                P.op("dve", lambda e: e.tensor_scalar(r[:], r[:], rp[:, 0:1], 0.25, ALU.mult, ALU.add), reads=["r_r", "r_rp"], writes=["r_r"])
            P.op("dve", lambda e: e.tensor_copy(ki[:], r[:]), reads=["r_r"], writes=["r_ki"])
            P.op("dve", lambda e: e.tensor_copy(kf[:], ki[:]), reads=["r_ki"], writes=["r_kf"])
            P.op("dve", lambda e: e.tensor_tensor(f[:], r[:], kf[:], ALU.subtract), reads=["r_r", "r_kf"], writes=["r_f"])
            P.op("dve", lambda e: e.tensor_single_scalar(m[:], f[:], 0.5, ALU.is_gt), reads=["r_f"], writes=["r_m"])
            P.op("dve", lambda e: e.tensor_tensor(f[:], f[:], m[:], ALU.subtract), reads=["r_f", "r_m"], writes=["r_f"])
            P.op("dve", lambda e: e.tensor_single_scalar(m[:], f[:], -0.5, ALU.is_lt), reads=["r_f"], writes=["r_m"])
            P.op("dve", lambda e: e.tensor_tensor(f[:], f[:], m[:], ALU.add), reads=["r_f", "r_m"], writes=["r_f"])
            P.op("act", lambda e: e.activation(kf[:], f[:], AF.Sin, scale=6.28318), reads=["r_f"], writes=["r_kf"])
            if which == 0:
                P.op("dve", lambda e: e.tensor_scalar(kf[:], kf[:], rp[:, 1:2], None, ALU.mult), reads=["r_kf", "r_rp"], writes=["r_kf"])
                P.dma(lambda e: e.dma_start(out=C.ropeS, in_=kf[:]), reads=["r_kf"], writes=["ropeS"])
            else:
                P.dma(lambda e: e.dma_start(out=C.ropeC, in_=kf[:]), reads=["r_kf"], writes=["ropeC"])


def _proj_blocks():
    blks = []
    for i in range(4):
        blks.append((i * 128, 128, i * 128, 0.125, "rope", "QdT", i * 128))
    for i in range(4):
        blks.append((512 + i * 128, 128, 512 + i * 128, 1.0, "rope", "KdT", i * 128))
    for i in range(4):
        blks.append((1536 + i * 128, 128, 1024 + i * 128, 0.125, "rope", "QnT", i * 128))
    blks.append((2048, 128, None, 1.0, "copy", "kcrT", 0))
    blks.append((2176, 128, None, 1.0, "copy", "vcrT", 0))
    blks.append((2304, 128, 1536, 1.0, "rope", "ksT", 0))
    blks.append((2560, 128, 1664, 1.0, "rope", "kwT", 0))
    for i in range(8):
        blks.append((2840 + i * 128, 128, None, 1.0, "sig", "gaT", i * 128))
    for i in range(8):
        blks.append((3864 + i * 128, 128, None, 1.0, "sig", "gbT", i * 128))
    return blks


def phase_proj(C, l):
    P, nc = C.P, C.nc
    win_d = C.w["w_in"][l]
    wpm_d = C.w["w_in_perm"][l]
    with ExitStack() as st:
        sb, ps = _alloc(C, st)
        win = sb("p_win", [128, 8, IN_COLS], BF16)
        wpm = sb("p_wpm", [128, 8, NPERM], BF16)
        gain = sb("p_gain", [128, D], F32)
        ident = sb("p_ident", [128, 128], BF16)
        hin2 = [sb("p_hin%d" % i, [128, 4, D], F32) for i in range(2)]
        junk = sb("p_junk", [128, D], BF16)
        xn = [sb("p_xn%d" % i, [128, D], BF16) for i in range(2)]
        uT2 = [sb("p_uT%d" % i, [128, 8, 512], BF16) for i in range(2)]
        ss = sb("p_ss", [128, 4], F32)
        rstd = sb("p_rstd", [128, 4], F32)
        Cs2 = [sb("p_C%d" % i, [128, 512], F32) for i in range(2)]
        Ss2 = [sb("p_S%d" % i, [128, 512], F32) for i in range(2)]
        t1 = [sb("p_t1%d" % i, [128, 512], F32) for i in range(2)]
        t2 = [sb("p_t2%d" % i, [128, 512], F32) for i in range(2)]
        stb = [sb("p_stb%d" % i, [128, 512], BF16) for i in range(3)]
        stf = [sb("p_stf%d" % i, [128, 512], F32) for i in range(3)]
        pa = [ps("p_pa%d" % i, [128, 512], F32) for i in range(2)]
        pb = [ps("p_pb%d" % i, [128, 512], F32) for i in range(2)]
        pv = [ps("p_pv%d" % i, [128, 512], F32) for i in range(2)]
        pt = ps("p_pt", [128, 8, 128], BF16)
        load_const(C, ident[:], "ident_bf", "p_ident")
        P.dma(lambda e: e.dma_start(out=gain[:], in_=C.w["mix_norm"][l:l + 1, :].partition_broadcast(128)), writes=["p_gain"])
        WCH = [(0, 512), (512, 1024), (1024, 1536), (1536, 2048), (2048, 2840), (2840, 3864), (3864, 4888)]
        PCH = [(0, 512), (512, 1024), (1024, 1536), (1536, 1792)]

        def ld(t, d, lo, hi, key):
            P.dma(lambda e: e.dma_start(out=t[:, :, lo:hi], in_=d[:, lo:hi].rearrange("(kc p) f -> p kc f", p=128)),
                  writes=[key], q="pool")
        for kind, ci in (("w", 0), ("p", 0), ("w", 1), ("p", 1), ("w", 3), ("p", 2), ("w", 4), ("p", 3), ("w", 5), ("w", 6), ("w", 2)):
            if kind == "w":
                ld(win, win_d, WCH[ci][0], WCH[ci][1], ("p_win", ci))
            else:
                ld(wpm, wpm_d, PCH[ci][0], PCH[ci][1], ("p_wpm", ci))

        def wkey(c0):
            return [("p_win", [i for i, (a, b_) in enumerate(WCH) if a <= c0 < b_][0])]

        def pkey(c0):
            return [("p_wpm", [i for i, (a, b_) in enumerate(PCH) if a <= c0 < b_][0])]
        wk = [("p_win", i) for i in range(len(WCH))]
        blks = _proj_blocks()
        nb = 0
        nf = 0
        def load_stats(g):
            nb_ = g % 2
            r0_ = g * 512
            P.dma(lambda e: e.dma_start(out=hin2[nb_][:], in_=C.hbuf[r0_:r0_ + 512, :].rearrange("(t p) d -> p t d", p=128)),
                  reads=[("hdram", g)], writes=[("p_hin%d" % nb_, tt) for tt in range(4)])
            P.dma(lambda e: e.dma_start(out=Cs2[nb_][:], in_=C.ropeC[:, r0_:r0_ + 512]), reads=["ropeC"], writes=[("p_C", nb_)])
            P.dma(lambda e: e.dma_start(out=Ss2[nb_][:], in_=C.ropeS[:, r0_:r0_ + 512]), reads=["ropeS"], writes=[("p_S", nb_)])
            norm_stats(C, "p_", hin2[nb_], ss, rstd, junk, hkey="p_hin%d" % nb_)

        def ntile(g, tt):
            nb_ = g % 2
            norm_tile(C, "p_", tt, hin2[nb_], gain, ident, rstd, xn, pt, uT2[nb_], hkey="p_hin%d" % nb_, xkey="p_xnT%d" % nb_)

        for g in range(NG):
            r0 = g * 512
            gb_ = g % 2
            uT, Cs, Ss = uT2[gb_], Cs2[gb_], Ss2[gb_]
            if g == 0:
                load_stats(0)
                for tt in range(4):
                    ntile(0, tt)
            xk = [("p_xnT%d" % gb_, tt) for tt in range(4)]
            for bi, (c0, ncol, pc0, scale, kind, dname, drow) in enumerate(blks):
                b = bi % 2
                dst = getattr(C, dname)
                if g + 1 < NG:
                    if bi == 16:
                        load_stats(g + 1)
                    if bi in (18, 22, 26, 30):
                        ntile(g + 1, (bi - 18) // 4)

                def mm(e, wt, col0, ncol, dstp, uT=uT):
                    ins = None
                    for kc in range(8):
                        ins = e.matmul(dstp[0:ncol, :], wt[:, kc, col0:col0 + ncol], uT[:, kc, :], start=(kc == 0), stop=(kc == 7))
                    return ins
                P.op("pe", lambda e, c0=c0, ncol=ncol, b=b, mm=mm: mm(e, win, c0, ncol, pa[b]), reads=xk + wkey(c0), writes=[("p_pa", b)])
                if kind == "rope":
                    P.op("pe", lambda e, pc0=pc0, ncol=ncol, b=b, mm=mm: mm(e, wpm, pc0, ncol, pb[b]), reads=xk + pkey(pc0), writes=[("p_pb", b)])
                    P.op("dve", lambda e, b=b, scale=scale, Cs=Cs: e.scalar_tensor_tensor(t1[b][:], pa[b][:], scale, Cs[:], ALU.mult, ALU.mult),
                         reads=[("p_pa", b), ("p_C", gb_)], writes=[("p_t1", b)])
                    P.op("dve", lambda e, b=b, scale=scale, Ss=Ss: e.scalar_tensor_tensor(t2[b][:], pb[b][:], scale, Ss[:], ALU.mult, ALU.mult),
                         reads=[("p_pb", b), ("p_S", gb_)], writes=[("p_t2", b)])
                    sbi = nb % 3
                    nb += 1
                    P.op("pool", lambda e, b=b, sbi=sbi: e.tensor_tensor(stb[sbi][:], t1[b][:], t2[b][:], ALU.add),
                         reads=[("p_t1", b), ("p_t2", b)], writes=[("p_stb", sbi)])
                    P.dma(lambda e, dst=dst, drow=drow, r0=r0, sbi=sbi: e.dma_start(out=dst[drow:drow + 128, r0:r0 + 512], in_=stb[sbi][:]),
                          reads=[("p_stb", sbi)], writes=[(dname, drow, g)])
                elif kind == "copy":
                    sbi = nb % 3
                    nb += 1
                    P.op("act", lambda e, b=b, sbi=sbi: e.copy(stb[sbi][:], pa[b][:]), reads=[("p_pa", b)], writes=[("p_stb", sbi)])
                    P.dma(lambda e, dst=dst, drow=drow, r0=r0, sbi=sbi: e.dma_start(out=dst[drow:drow + 128, r0:r0 + 512], in_=stb[sbi][:]),
                          reads=[("p_stb", sbi)], writes=[(dname, drow, g)])
                else:
                    sfi = nf % 3
                    nf += 1
                    P.op("act", lambda e, b=b, sfi=sfi, ncol=ncol: e.activation(stf[sfi][0:ncol, :], pa[b][0:ncol, :], AF.Sigmoid),
                         reads=[("p_pa", b)], writes=[("p_stf", sfi)])
                    P.dma(lambda e, dst=dst, drow=drow, r0=r0, sfi=sfi, ncol=ncol: e.dma_start(out=dst[drow:drow + ncol, r0:r0 + 512], in_=stf[sfi][0:ncol, :]),
                          reads=[("p_stf", sfi)], writes=[(dname, drow, g)])
            for tt in range(4):
                b = tt % 2
                rr = r0 + tt * 128

                def mmv(e, col0, ncol, dstp, tt=tt, uT=uT):
                    ins = None
                    for kc in range(8):
                        ins = e.matmul(dstp, uT[:, kc, tt * 128:(tt + 1) * 128], win[:, kc, col0:col0 + ncol], start=(kc == 0), stop=(kc == 7))
                    return ins
                P.op("pe", lambda e, b=b, mmv=mmv: mmv(e, 1024, 512, pv[b][:, :]), reads=xk + wk, writes=[("p_pv", b)])
                sbi = nb % 3
                nb += 1
                P.op("act", lambda e, b=b, sbi=sbi: e.copy(stb[sbi][:], pv[b][:]), reads=[("p_pv", b)], writes=[("p_stb", sbi)])
                P.dma(lambda e, rr=rr, sbi=sbi: e.dma_start(out=C.Vd[rr:rr + 128, :], in_=stb[sbi][:]),
                      reads=[("p_stb", sbi)], writes=[("Vd", g, tt)])

                def mm2(e, b=b, mmv=mmv):
                    mmv(e, 2432, 128, pb[b][:, 0:128])
                    mmv(e, 2688, 128, pb[b][:, 128:256])
                    return mmv(e, 2816, 24, pb[b][:, 256:280])
                P.op("pe", mm2, reads=xk + wk, writes=[("p_pb", b)])
                sbi = nb % 3
                nb += 1
                P.op("act", lambda e, b=b, sbi=sbi: e.copy(stb[sbi][:, 0:256], pb[b][:, 0:256]), reads=[("p_pb", b)], writes=[("p_stb", sbi)])
                P.dma(lambda e, rr=rr, sbi=sbi: e.dma_start(out=C.vs[rr:rr + 128, :], in_=stb[sbi][:, 0:128]),
                      reads=[("p_stb", sbi)], writes=[("vs", g, tt)])
                P.dma(lambda e, rr=rr, sbi=sbi: e.dma_start(out=C.vw[rr:rr + 128, :], in_=stb[sbi][:, 128:256]),
                      reads=[("p_stb", sbi)], writes=[("vw", g, tt)])
                sfi = nf % 3
                nf += 1
                P.op("act", lambda e, b=b, sfi=sfi: e.activation(stf[sfi][:, 0:24], pb[b][:, 256:280], AF.Sigmoid),
                     reads=[("p_pb", b)], writes=[("p_stf", sfi)])
                P.dma(lambda e, rr=rr, sfi=sfi: e.dma_start(out=C.gntok[rr:rr + 128, :], in_=stf[sfi][:, 0:24]),
                      reads=[("p_stf", sfi)], writes=[("gntok", g, tt)])


def phase_cmp(C, l):
    P, nc = C.P, C.nc
    with ExitStack() as st:
        sb, ps = _alloc(C, st)
        raw = [sb("c_raw%d" % i, [128, T], BF16) for i in range(2)]
        w1 = [sb("c_w1%d" % i, [128, 32, 256], BF16) for i in range(2)]
        w2 = [sb("c_w2%d" % i, [128, 2, 64], BF16) for i in range(2)]
        posT = [sb("c_posT%d" % i, [128, 32], BF16) for i in range(2)]
        bias = [sb("c_bias%d" % i, [128, 2], F32) for i in range(2)]
        hT = [sb("c_hT%d" % i, [128, 2, 256], BF16) for i in range(2)]
        ko = [sb("c_ko%d" % i, [128, 256], BF16) for i in range(2)]
        vo = [sb("c_vo%d" % i, [128, 64], BF16) for i in range(2)]
        pbias = ps("c_pbias", [128, 2], F32)
        ph = [ps("c_ph%d" % i, [128, 256], F32) for i in range(2)]
        pk = ps("c_pk", [128, 256], F32)
        pvv = [ps("c_pvv%d" % i, [128, 64], F32) for i in range(2)]
        P.dma(lambda e: e.dma_start(out=raw[0][:], in_=C.kcrT), reads=["kcrT_all"], writes=[("c_raw", 0)])
        P.dma(lambda e: e.dma_start(out=raw[1][:], in_=C.vcrT), reads=["vcrT_all"], writes=[("c_raw", 1)])
        for kv in range(2):
            w1_d = C.w["cmp_w1"][l, kv].rearrange("(l d) h -> d l h", d=64)
            for hf in range(2):
                P.dma(lambda e, kv=kv, hf=hf, w1_d=w1_d: e.dma_start(out=w1[kv][hf * 64:(hf + 1) * 64, :, :], in_=w1_d),
                      writes=[("c_w1", kv, hf)], q="pool")
                P.dma(lambda e, kv=kv, hf=hf: e.dma_start(out=posT[kv][hf * 64:(hf + 1) * 64, :],
                                                         in_=C.w["cmp_pos"][l, kv].rearrange("l d -> d l"),
                                                         allow_slow_non_contiguous=True),
                      writes=[("c_posT", kv, hf)], q="pool")
            P.dma(lambda e, kv=kv: e.dma_start(out=w2[kv][:], in_=C.w["cmp_w2"][l, kv].rearrange("(c p) o -> p c o", p=128)),
                  writes=[("c_w2", kv)], q="pool")
            P.op("dve", lambda e, kv=kv: e.memset(hT[kv][:], 0.0), writes=[("c_hT", kv)])
        cnt = 0
        for kv in range(2):
            def fb(e, kv=kv):
                ins = None
                for hc in range(2):
                    for li in range(32):
                        ins = e.matmul(pbias[:, hc:hc + 1], w1[kv][0:64, li, hc * 128:(hc + 1) * 128], posT[kv][0:64, li:li + 1],
                                       start=(li == 0), stop=(li == 31))
                return ins
            P.op("pe", fb, reads=[("c_w1", kv, 0), ("c_posT", kv, 0)], writes=["c_pbias"])
            P.op("dve", lambda e, kv=kv: e.tensor_copy(bias[kv][:], pbias[:]), reads=["c_pbias"], writes=[("c_bias", kv)])
            for g in range(2):
                lo, hi = g * 64, (g + 1) * 64
                for hc in range(2):
                    def fh(e, kv=kv, hc=hc, lo=lo, hi=hi):
                        ins = None
                        for li in range(32):
                            ins = e.matmul(ph[hc][:, 0:255], w1[kv][lo:hi, li, hc * 128:(hc + 1) * 128],
                                           raw[kv][lo:hi, li:li + 4065:16], start=(li == 0), stop=(li == 31))
                        return ins
                    P.op("pe", fh, reads=[("c_w1", kv, g), ("c_raw", kv)], writes=[("c_ph", hc)])
                    P.op("act", lambda e, kv=kv, hc=hc: e.activation(hT[kv][:, hc, 0:255], ph[hc][:, 0:255], AF.Silu,
                                                                     bias=bias[kv][:, hc:hc + 1]),
                         reads=[("c_ph", hc), ("c_bias", kv)], writes=[("c_hT", kv)])
                if kv == 0:
                    def fk(e):
                        ins = None
                        for hc in range(2):
                            ins = e.matmul(pk[0:64, :], w2[0][:, hc, :], hT[0][:, hc, :], start=(hc == 0), stop=(hc == 1))
                        return ins
                    P.op("pe", fk, reads=[("c_w2", 0), ("c_hT", 0)], writes=["c_pk"])
                    P.op("dve", lambda e, g=g: e.tensor_copy(ko[g][0:64, :], pk[0:64, :]), reads=["c_pk"], writes=[("c_ko", g)])
                    P.dma(lambda e, g=g, lo=lo, hi=hi: e.dma_start(out=C.kcT[lo:hi, :], in_=ko[g][0:64, :]),
                          reads=[("c_ko", g)], writes=[("kcT", g)])
                else:
                    for ct in range(2):
                        def fv(e, ct=ct):
                            ins = None
                            for hc in range(2):
                                ins = e.matmul(pvv[ct][:], hT[1][:, hc, ct * 128:(ct + 1) * 128], w2[1][:, hc, :],
                                               start=(hc == 0), stop=(hc == 1))
                            return ins
                        P.op("pe", fv, reads=[("c_w2", 1), ("c_hT", 1)], writes=[("c_pvv", ct)])
                        P.op("dve", lambda e, ct=ct: e.tensor_copy(vo[ct][:], pvv[ct][:]), reads=[("c_pvv", ct)], writes=[("c_vo", ct)])
                        P.dma(lambda e, g=g, ct=ct: e.dma_start(out=C.vc[g, ct * 128:(ct + 1) * 128, :], in_=vo[ct][:]),
                              reads=[("c_vo", ct)], writes=[("vc", g, ct)])


class AttnShared:
    pass


def attn_setup(C, sb, ps, pfx):
    A = AttnShared()
    A.S, A.O, A.SUM = [], [], []
    for i in range(2):
        A.S.append(ps(pfx + "S%d" % i, [128, 512], F32))
        A.O.append(ps(pfx + "O%d" % i, [128, 512], F32))
        A.SUM.append(ps(pfx + "SUM%d" % i, [128, 512], F32))
    A.Pt = [sb(pfx + "Pt%d" % i, [128, 512], BF16) for i in range(2)]
    A.ones = sb(pfx + "ones", [128, 128], BF16)
    A.masks = sb(pfx + "masks", [128, 13, 512], BF16)
    A.ident = sb(pfx + "identb", [128, 128], BF16)
    load_const(C, A.ones[:], "ones_bf", "a_ones")
    load_const(C, A.masks[:], "masks", "a_masks")
    load_const(C, A.ident[:], "ident_bf", "a_ident")
    A.base = 0
    A.calls = 0
    return A


def attn_call(C, A, qT, kT_fn, v_fn, dv, steps, rkeys, step_hook=None):
    P = C.P
    par = A.calls % 2
    A.calls += 1
    n = len(steps)
    base = A.base
    A.base += n

    def S_op(j):
        kt, masks, c0, c1 = steps[j]
        sp = (base + j) % 2

        def f(e):
            ins = e.matmul(A.S[sp][:, c0:c1], kT_fn(kt), qT[:, c0:c1], start=True, stop=(len(masks) == 0))
            for i, (lh, rh) in enumerate(masks):
                ins = e.matmul(A.S[sp][:, c0:c1], lh, rh[:, c0:c1], start=False, stop=(i == len(masks) - 1))
            return ins
        P.op("pe", f, reads=list(rkeys) + ["a_masks", "a_ident"], writes=[("a_S", sp)])

    def E_op(j):
        kt, masks, c0, c1 = steps[j]
        sp = (base + j) % 2
        P.op("act", lambda e: e.activation(A.Pt[sp][:, c0:c1], A.S[sp][:, c0:c1], AF.Exp), reads=[("a_S", sp)], writes=[("a_Pt", sp)])

    def PV_op(j):
        kt, _, c0, c1 = steps[j]
        sp = (base + j) % 2

        def f(e):
            e.matmul(A.O[par][0:dv, c0:c1], v_fn(kt), A.Pt[sp][:, c0:c1], start=(j == 0), stop=(j == n - 1))
            return e.matmul(A.SUM[par][0:dv, c0:c1], A.ones[:, 0:dv], A.Pt[sp][:, c0:c1], start=(j == 0), stop=(j == n - 1))
        P.op("pe", f, reads=list(rkeys) + [("a_Pt", sp), "a_ones"], writes=[("a_O", par), ("a_SUM", par)])
        if step_hook is not None:
            step_hook(j, sp)

    S_op(0)
    for j in range(n):
        E_op(j)
        if j + 1 < n:
            S_op(j + 1)
        PV_op(j)
    return par


def attn_setup2(C, sb, ps, pfx, ns=3):
    A = AttnShared()
    A.S, A.OTA, A.OTB = [], [], []
    for i in range(2):
        A.S.append(ps(pfx + "S%d" % i, [128, 512], F32))
        A.OTA.append(ps(pfx + "OTA%d" % i, [128, 512], F32))
        A.OTB.append(ps(pfx + "OTB%d" % i, [128, 512], F32))
    for i in range(2, ns):
        A.S.append(ps(pfx + "S%d" % i, [128, 512], F32))
    A.Pt = [sb(pfx + "Pt%d" % i, [128, 512], BF16) for i in range(ns)]
    A.masks = sb(pfx + "masks", [128, 13, 512], BF16)
    A.ident = sb(pfx + "identb", [128, 128], BF16)
    load_const(C, A.masks[:], "masks", "a_masks")
    load_const(C, A.ident[:], "ident_bf", "a_ident")
    A.base = 0
    A.calls = 0
    A.queue = []
    return A


def ot_ap(A, par, qt, wide, c0=0, c1=None):
    if not wide:
        return ("A", par), A.OTA[par], qt * 128
    bank = A.OTA[par] if qt < 2 else A.OTB[par]
    return ("A" if qt < 2 else "B", par), bank, (qt % 2) * 129


class Call:
    pass


def attn_call2(C, A, qT, kT_fn, vp_fn, nv, steps, rkeys, wide, epi=None):
    c = Call()
    c.qT, c.kT_fn, c.vp_fn, c.nv, c.steps, c.rkeys, c.wide, c.epi = qT, kT_fn, vp_fn, nv, steps, list(rkeys), wide, epi
    A.queue.append(c)
    return c


def emit_calls(C, A, look=2):
    P = C.P
    calls = A.queue
    A.queue = []
    NS, NP = len(A.S), len(A.Pt)
    flat = []
    for ci, c in enumerate(calls):
        c.par = A.calls % 2
        A.calls += 1
        writes = []
        for j in range(len(c.steps)):
            _, _, c0, c1 = c.steps[j]
            for qt in range(c0 // 128, c1 // 128):
                writes.append((j, qt, ot_ap(A, c.par, qt, c.wide)[0]))
        c.first, c.last = {}, {}
        for j, qt, b in writes:
            c.first.setdefault(b, (j, qt))
            c.last[b] = (j, qt)
        for j in range(len(c.steps)):
            flat.append((c, j))
    base = A.base
    A.base += len(flat)

    def S_op(i):
        c, j = flat[i]
        kt, masks, c0, c1 = c.steps[j]
        sp = (base + i) % NS

        def f(e):
            ins = e.matmul(A.S[sp][:, c0:c1], c.kT_fn(kt), c.qT[:, c0:c1], start=True, stop=(len(masks) == 0))
            for k, (lh, rh) in enumerate(masks):
                ins = e.matmul(A.S[sp][:, c0:c1], lh, rh[:, c0:c1], start=False, stop=(k == len(masks) - 1))
            return ins
        P.op("pe", f, reads=c.rkeys + ["a_masks", "a_ident"], writes=[("a_S", sp)])

    def E_op(i):
        c, j = flat[i]
        kt, masks, c0, c1 = c.steps[j]
        sp = (base + i) % NS
        pp = (base + i) % NP
        P.op("act", lambda e: e.activation(A.Pt[pp][:, c0:c1], A.S[sp][:, c0:c1], AF.Exp), reads=[("a_S", sp)], writes=[("a_Pt", pp)])

    def PV_op(i):
        c, j = flat[i]
        kt, _, c0, c1 = c.steps[j]
        pp = (base + i) % NP
        plan = []
        for qt in range(c0 // 128, c1 // 128):
            b, bank, off = ot_ap(A, c.par, qt, c.wide)
            plan.append((qt, bank, off, c.first[b] == (j, qt), c.last[b] == (j, qt)))
        nv = c.nv

        def f(e):
            ins = None
            for qt, bank, off, st_, sp_ in plan:
                ins = e.matmul(bank[:, off:off + nv], A.Pt[pp][:, qt * 128:(qt + 1) * 128], c.vp_fn(kt), start=st_, stop=sp_)
            return ins
        P.op("pe", f, reads=c.rkeys + [("a_Pt", pp)], writes=[("a_OT", c.par)])
        if j == len(c.steps) - 1 and c.epi is not None:
            c.epi(c.par)

    n = len(flat)
    for i in range(min(look, n)):
        S_op(i)
    for i in range(n):
        E_op(i)
        if i + look < n:
            S_op(i + look)
        PV_op(i)


def phase_diff(C, l):
    P, nc = C.P, C.nc
    lam_init = 0.8 - 0.6 * math.exp(-0.3 * l)
    with ExitStack() as st:
        sb, ps = _alloc(C, st)
        A = attn_setup2(C, sb, ps, "d_", ns=4)
        Q = [[sb("d_Q%d_%d" % (i, m), [128, T], BF16) for m in range(2)] for i in range(2)]
        Kt = [sb("d_K%d" % i, [128, T], BF16) for i in range(2)]
        V = [sb("d_V%d" % i, [128, 32, 129], BF16) for i in range(2)]
        for i in range(2):
            P.op("pool", lambda e, i=i: e.memset(Q[i][0][64:128, :], 0.0), writes=[("d_Qz", i, 0)])
            P.op("pool", lambda e, i=i: e.memset(Q[i][1][0:64, :], 0.0), writes=[("d_Qz", i, 1)])
        lp = sb("d_lp", [128, 256], F32)
        prod = sb("d_prod", [128, 128], F32)
        s12 = sb("d_s12", [128, 2], F32)
        neglam = sb("d_neglam", [128, 1], F32)
        rs4 = sb("d_rs4", [128, 4], F32)
        t4 = sb("d_t4", [128, 4], F32)
        ss4 = sb("d_ss4", [128, 4], F32)
        c4 = sb("d_c4", [128, 4], F32)
        On0 = sb("d_On0", [128, 4, 128], F32)
        of = sb("d_of", [128, 4, 128], F32)
        sq = sb("d_sq", [128, 4, 128], F32)
        ob = [sb("d_ob%d" % i, [128, 4, 128], BF16) for i in range(2)]
        epsb = sb("d_epsb", [128, 1], F32)
        P.op("dve", lambda e: e.memset(epsb[:], EPS), writes=["d_epsb"])
        for i in range(2):
            P.op("pool", lambda e, i=i: e.memset(V[i][:, :, 128:129], 1.0), writes=[("d_V1", i)])
        P.dma(lambda e: e.dma_start(out=lp[:], in_=C.w["diff_lambda"][l:l + 1, :].partition_broadcast(128)), writes=["d_lp"])
        P.op("dve", lambda e: e.tensor_tensor(prod[:, 0:64], lp[:, 0:64], lp[:, 64:128], ALU.mult), reads=["d_lp"], writes=["d_prod0"])
        P.op("dve", lambda e: e.tensor_tensor(prod[:, 64:128], lp[:, 128:192], lp[:, 192:256], ALU.mult), reads=["d_lp"], writes=["d_prod1"])
        P.op("dve", lambda e: e.reduce_sum(s12[:, 0:1], prod[:, 0:64], axis=AX.X), reads=["d_prod0"], writes=["d_s0"])
        P.op("dve", lambda e: e.reduce_sum(s12[:, 1:2], prod[:, 64:128], axis=AX.X), reads=["d_prod1"], writes=["d_s1"])
        P.op("act", lambda e: e.activation(s12[:], s12[:], AF.Exp), reads=["d_s0", "d_s1"], writes=["d_e12"])
        P.op("dve", lambda e: e.tensor_tensor(neglam[:], s12[:, 1:2], s12[:, 0:1], ALU.subtract), reads=["d_e12"], writes=["d_neglam"])
        P.op("dve", lambda e: e.tensor_scalar(neglam[:], neglam[:], -lam_init, None, ALU.add), reads=["d_neglam"], writes=["d_neglam"])
        nob = 0

        def sums(par, dst):
            P.op("dve", lambda e: e.tensor_scalar(dst[:, 0:2], A.OTA[par][:, 128:258:129], 1e-20, None, ALU.max),
                 reads=[("a_OT", par)], writes=["d_rs4"])
            P.op("dve", lambda e: e.tensor_scalar(dst[:, 2:4], A.OTB[par][:, 128:258:129], 1e-20, None, ALU.max),
                 reads=[("a_OT", par)], writes=["d_rs4"])
            P.op("dve", lambda e: e.reciprocal(dst[:], dst[:]), reads=["d_rs4"], writes=["d_rs4"])

        for h in range(4):
            hb = h % 2
            P.dma(lambda e, h=h, hb=hb: e.dma_start(out=Q[hb][0][0:64, :], in_=C.QdT[h * 128:h * 128 + 64, :]), reads=["QdT_all"], writes=[("d_Q", hb, 0)])
            P.dma(lambda e, h=h, hb=hb: e.dma_start(out=Q[hb][1][64:128, :], in_=C.QdT[h * 128 + 64:(h + 1) * 128, :]), reads=["QdT_all"], writes=[("d_Q", hb, 1)])
            P.dma(lambda e, h=h, hb=hb: e.dma_start(out=Kt[hb][:], in_=C.KdT[h * 128:(h + 1) * 128, :]), reads=["KdT_all"], writes=[("d_K", hb)])
            P.dma(lambda e, h=h, hb=hb: e.dma_start(out=V[hb][:, :, 0:128], in_=C.Vd[:, h * 128:(h + 1) * 128].rearrange("(k p) c -> p k c", p=128)),
                  reads=["Vd_all", ("d_V1", hb)], writes=[("d_V", hb)])
            for qg in QORDER:
                q0 = qg * 512
                for m in range(2):
                    lo, hi = m * 64, (m + 1) * 64
                    steps = []
                    for kt in range(4 * qg + 4):
                        rel = kt - 4 * qg
                        mk = [(A.ident[:], A.masks[:, rel, :])] if rel >= 0 else []
                        steps.append((kt, mk, max(rel, 0) * 128, 512))
                    fin = None
                    if m == 1:
                        def fin(h=h, q0=q0, qg=qg):
                            nonlocal nob
                            ofk = [("d_of", qt) for qt in range(4)]
                            P.op("pool", lambda e: e.tensor_tensor(sq[:], of[:], of[:], ALU.mult), reads=ofk, writes=["d_sq"])
                            P.op("dve", lambda e: e.reduce_sum(ss4[:], sq[:], axis=AX.X), reads=["d_sq"], writes=["d_ss4"])
                            P.op("act", lambda e: e.activation(c4[:], ss4[:], AF.Ln, bias=epsb[:, 0:1], scale=1.0 / 128), reads=["d_ss4", "d_epsb"], writes=["d_c4"])
                            P.op("act", lambda e: e.activation(c4[:], c4[:], AF.Exp, scale=-0.5), reads=["d_c4"], writes=["d_c4"])
                            oi = nob % 2
                            nob += 1
                            for qt in range(4):
                                P.op("dve", lambda e, qt=qt, oi=oi: e.tensor_scalar(ob[oi][:, qt, :], of[:, qt, :], c4[:, qt:qt + 1], 1.0 - lam_init, ALU.mult, ALU.mult),
                                     reads=[("d_of", qt), "d_c4"], writes=[("d_ob", oi, qt)])
                            P.dma(lambda e, oi=oi: e.dma_start(
                                out=C.oatok[q0:q0 + 512, h * 128:(h + 1) * 128].rearrange("(t p) c -> p t c", p=128), in_=ob[oi][:]),
                                reads=[("d_ob", oi, qt) for qt in range(4)], writes=[("oatok", h, qg)])
                    def epi(par, m=m, fin=fin):
                        sums(par, rs4)
                        if m == 1:
                            P.op("dve", lambda e: e.tensor_scalar(t4[:], rs4[:], neglam[:, 0:1], None, ALU.mult),
                                 reads=["d_rs4", "d_neglam"], writes=["d_t4"])
                        for qt in range(4):
                            _, bank, off = ot_ap(A, par, qt, True)
                            if m == 0:
                                P.op("dve", lambda e, qt=qt, bank=bank, off=off: e.tensor_scalar(On0[:, qt, :], bank[:, off:off + 128], rs4[:, qt:qt + 1], None, ALU.mult),
                                     reads=[("a_OT", par), "d_rs4"], writes=[("d_On0", qt)])
                            else:
                                P.op("dve", lambda e, qt=qt, bank=bank, off=off: e.scalar_tensor_tensor(of[:, qt, :], bank[:, off:off + 128], t4[:, qt:qt + 1], On0[:, qt, :], ALU.mult, ALU.add),
                                     reads=[("a_OT", par), "d_t4", ("d_On0", qt)], writes=[("d_of", qt)])
                        if m == 1:
                            fin()
                    attn_call2(C, A, Q[hb][m][:, q0:q0 + 512],
                               lambda kt, hb=hb: Kt[hb][:, kt * 128:(kt + 1) * 128],
                               lambda kt, hb=hb: V[hb][:, kt, :], 129, steps,
                               [("d_Q", hb, m), ("d_Qz", hb, m), ("d_K", hb), ("d_V", hb), ("d_V1", hb)], True, epi)
            emit_calls(C, A, look=3)


def phase_nsa(C, l):
    P, nc = C.P, C.nc
    CMP_MASK = {31: 8, -481: 9, -993: 10, -1505: 11, -2017: 12}
    with ExitStack() as st:
        sb, ps = _alloc(C, st)
        A = attn_setup2(C, sb, ps, "n_")
        G = ps("n_G", [128, 512], F32)
        impAB = sb("n_impAB", [128, 2, 32, 64], F32)
        identf = sb("n_identf", [128, 128], F32)
        gtok = sb("n_gtok", [128, 32, 24], F32)
        QS = [sb("n_QS%d" % i, [128, 4, 512], BF16) for i in range(2)]
        ks = sb("n_ks", [128, T], BF16)
        kw = sb("n_kw", [128, T], BF16)
        vsp = sb("n_vsp", [128, 32, 65], BF16)
        vwp = sb("n_vwp", [128, 32, 65], BF16)
        kc = sb("n_kc", [128, 256], BF16)
        ovvc = sb("n_ovvc", [128, 2, 129], BF16)
        for i in range(2):
            P.op("pool", lambda e, i=i: e.memset(QS[i][64:128, :, :], 0.0), writes=[("n_QSs", i, hh) for hh in range(4)])
        P.op("pool", lambda e: e.memset(kw[64:128, :], 0.0), writes=["n_kwz"])
        P.op("pool", lambda e: e.memset(kc[64:128, :], 0.0), writes=["n_kcz"])
        rst = sb("n_rst", [128, 4], F32)
        rs4 = sb("n_rs4", [128, 4], F32)
        w4 = sb("n_w4", [128, 4], F32)
        impacc = sb("n_impacc", [128, 4, 64], F32)
        impf = sb("n_impf", [128, 4, 64], F32)
        top8 = sb("n_top8", [128, 4, 8], F32)
        selm = sb("n_selm", [128, 4, 128], F32)
        acc = [[sb("n_acc%d_%d" % (j, i), [128, 4, 64], F32) for i in range(4)] for j in range(2)]
        ob = [sb("n_ob%d" % i, [128, 4, 64], BF16) for i in range(2)]
        P.op("dve", lambda e: e.memset(selm[:], 0.0), writes=[("n_selm", qt) for qt in range(4)])
        P.op("pool", lambda e: e.memset(vsp[:, :, 64:65], 1.0), writes=["n_vs1"])
        P.op("pool", lambda e: e.memset(vwp[:, :, 64:65], 1.0), writes=["n_vw1"])
        P.op("pool", lambda e: e.memset(ovvc[:, :, 128:129], 1.0), writes=["n_ov1c"])
        P.dma(lambda e: e.dma_start(out=ovvc[:, :, 0:64], in_=C.c["ov1"][:, :, 0:64]), reads=["n_ov1c"], writes=["n_ov"])
        P.dma(lambda e: e.dma_start(out=ks[64:128, :], in_=C.c["E"]), writes=["n_E"])
        load_const(C, impAB[:], "impAB", "n_impAB")
        load_const(C, identf[:], "ident_f", "n_identf")
        P.dma(lambda e: e.dma_start(out=gtok[:], in_=C.gntok.rearrange("(t p) c -> p t c", p=128)), reads=["gntok_all"], writes=["n_gtok"])
        nob = 0
        nq = 0

        def epilogue(hh, h, par, branch, qg, ab):
            r = h * 3 + branch
            P.op("dve", lambda e: e.tensor_scalar(rs4[:], A.OTA[par][:, 64:512:128], 1e-20, None, ALU.max), reads=[("a_OT", par)], writes=["n_rs4"])
            P.op("dve", lambda e: e.reciprocal(rs4[:], rs4[:]), reads=["n_rs4"], writes=["n_rs4"])
            P.op("dve", lambda e: e.tensor_tensor(w4[:], rs4[:], gtok[:, qg * 4:(qg + 1) * 4, r], ALU.mult), reads=["n_rs4", "n_gtok"], writes=["n_w4"])
            for qt in range(4):
                P.op("dve", lambda e, qt=qt: e.scalar_tensor_tensor(acc[ab][hh][:, qt, :], A.OTA[par][:, qt * 128:qt * 128 + 64], w4[:, qt:qt + 1],
                                                                     acc[ab][hh][:, qt, :], ALU.mult, ALU.add),
                     reads=[("a_OT", par), "n_w4", ("n_acc", ab, hh, qt)], writes=[("n_acc", ab, hh, qt)])

        for g in range(2):
            lo, hi = g * 64, (g + 1) * 64
            P.dma(lambda e, lo=lo, hi=hi: e.dma_start(out=ks[0:64, :], in_=C.ksT[lo:hi, :]), reads=["ksT_all"], writes=["n_ks"])
            P.dma(lambda e, lo=lo, hi=hi: e.dma_start(out=kw[0:64, :], in_=C.kwT[lo:hi, :]), reads=["kwT_all"], writes=["n_kw"])
            P.dma(lambda e, lo=lo, hi=hi: e.dma_start(out=vsp[:, :, 0:64], in_=C.vs[:, lo:hi].rearrange("(k p) c -> p k c", p=128)),
                  reads=["vs_all", "n_vs1"], writes=["n_vs"])
            P.dma(lambda e, lo=lo, hi=hi: e.dma_start(out=vwp[:, :, 0:64], in_=C.vw[:, lo:hi].rearrange("(k p) c -> p k c", p=128)),
                  reads=["vw_all", "n_vw1"], writes=["n_vw"])
            P.dma(lambda e, lo=lo, hi=hi: e.dma_start(out=kc[0:64, :], in_=C.kcT[lo:hi, :]), reads=["kcT_all"], writes=["n_kc"])
            P.dma(lambda e, g=g: e.dma_start(out=ovvc[:, :, 64:128], in_=C.vc[g].rearrange("(c p) o -> p c o", p=128)), reads=["vc_all", "n_ov1c", "n_ov"], writes=["n_vc"])
            def stage_a(qg, qb, ab, g=g):
                q0 = qg * 512
                P.dma(lambda e, g=g, q0=q0, qb=qb: e.dma_start(
                    out=QS[qb][0:64, :, :], in_=C.QnT[g * 256:(g + 1) * 256, q0:q0 + 512].rearrange("(h d) t -> d h t", d=64)),
                    reads=["QnT_all"], writes=[("n_QSq", qb)])
                csteps = []
                th0 = 31 - 512 * qg
                csteps.append((0, [(A.ident[:], A.masks[:, CMP_MASK[th0], :])] if th0 in CMP_MASK else [], 0, 512))
                if qg >= 4:
                    th1 = 2079 - 512 * qg
                    csteps.append((1, [(A.ident[:], A.masks[:, CMP_MASK[th1], :])], 0, 512))
                for hh in range(4):
                    h = 4 * g + hh
                    def epi_c(par, hh=hh, h=h, qg=qg, ab=ab):
                        P.op("dve", lambda e: e.tensor_scalar(rst[:, 0:2], A.OTA[par][:, 128:258:129], 1e-20, None, ALU.max),
                             reads=[("a_OT", par)], writes=["n_rst"])
                        P.op("dve", lambda e: e.tensor_scalar(rst[:, 2:4], A.OTB[par][:, 128:258:129], 1e-20, None, ALU.max),
                             reads=[("a_OT", par)], writes=["n_rst"])
                        P.op("dve", lambda e: e.reciprocal(rst[:], rst[:]), reads=["n_rst"], writes=["n_rst"])
                        P.op("dve", lambda e: e.tensor_tensor(w4[:], rst[:], gtok[:, qg * 4:(qg + 1) * 4, h * 3], ALU.mult),
                             reads=["n_rst", "n_gtok"], writes=["n_w4"])
                        for qt in range(4):
                            _, bank, off = ot_ap(A, par, qt, True)
                            if hh == 0:
                                P.op("dve", lambda e, qt=qt, bank=bank, off=off: e.tensor_scalar(impacc[:, qt, :], bank[:, off:off + 64], rst[:, qt:qt + 1], None, ALU.mult),
                                     reads=[("a_OT", par), "n_rst"], writes=[("n_impacc", qt)])
                            else:
                                P.op("dve", lambda e, qt=qt, bank=bank, off=off: e.scalar_tensor_tensor(impacc[:, qt, :], bank[:, off:off + 64], rst[:, qt:qt + 1],
                                                                                                         impacc[:, qt, :], ALU.mult, ALU.add),
                                     reads=[("a_OT", par), "n_rst", ("n_impacc", qt)], writes=[("n_impacc", qt)])
                            P.op("dve", lambda e, qt=qt, bank=bank, off=off: e.tensor_scalar(acc[ab][hh][:, qt, :], bank[:, off + 64:off + 128], w4[:, qt:qt + 1], None, ALU.mult),
                                 reads=[("a_OT", par), "n_w4"], writes=[("n_acc", ab, hh, qt)])
                    attn_call2(C, A, QS[qb][:, hh, :], lambda ct: kc[:, ct * 128:(ct + 1) * 128],
                               lambda ct: ovvc[:, ct, :], 129, csteps,
                               [("n_QSq", qb), "n_kc", "n_kcz", "n_vc", "n_ov", "n_ov1c", ("n_QSs", qb, hh)], True, epi_c)
                emit_calls(C, A)
                ik = [("n_impacc", qt) for qt in range(4)]
                P.op("dve", lambda e, qg=qg: e.tensor_tensor(impf[:], impacc[:], impAB[:, 0, qg * 4:(qg + 1) * 4, :], ALU.mult),
                     reads=ik + ["n_impAB"], writes=["n_impf"])
                P.op("dve", lambda e, qg=qg: e.tensor_tensor(impf[:], impf[:], impAB[:, 1, qg * 4:(qg + 1) * 4, :], ALU.add),
                     reads=["n_impf", "n_impAB"], writes=["n_impf"])
                for qt in range(4):
                    P.op("dve", lambda e, qt=qt: e.max(top8[:, qt, :], impf[:, qt, :]), reads=["n_impf"], writes=[("n_top8", qt)])
                for qt in range(4):
                    P.op("dve", lambda e, qt=qt: e.tensor_scalar(selm[:, qt, 64:128], impf[:, qt, :], top8[:, qt, 7:8], 1.0, ALU.is_ge, ALU.subtract),
                         reads=["n_impf", ("n_top8", qt)], writes=[("n_selm", qt)])

                def ftr(e):
                    ins = None
                    for qt in range(4):
                        ins = e.transpose(G[:, qt * 128:(qt + 1) * 128], selm[:, qt, :], identf[:])
                    return ins
                P.op("pe", ftr, reads=[("n_selm", qt) for qt in range(4)] + ["n_identf"], writes=["n_G"])
                for hh in range(4):
                    P.op("act", lambda e, qb=qb, hh=hh: e.activation(QS[qb][64:128, hh, :], G[64:128, :], AF.Copy, scale=-NEG),
                         reads=["n_G"], writes=[("n_QSs", qb, hh)])

            def stage_b(qg, qb, ab, g=g):
                q0 = qg * 512
                for hh in range(4):
                    h = 4 * g + hh
                    ssteps = []
                    for kt in range(4 * qg + 4):
                        rel = kt - 4 * qg
                        mk = []
                        if rel >= 0:
                            mk.append((A.ident[:], A.masks[:, rel, :]))
                        ssteps.append((kt, mk, max(rel, 0) * 128, 512))
                    attn_call2(C, A, QS[qb][:, hh, :], lambda kt: ks[:, kt * 128:(kt + 1) * 128],
                               lambda kt: vsp[:, kt, :], 65, ssteps,
                               [("n_QSq", qb), "n_ks", "n_vs", "n_vs1", "n_E", ("n_QSs", qb, hh)], False,
                               lambda par, hh=hh, h=h, qg=qg, ab=ab: epilogue(hh, h, par, 1, qg, ab))
                    wsteps = []
                    for kt in range(max(0, 4 * qg - 4), 4 * qg + 4):
                        rel = kt - 4 * qg
                        mi = rel if rel >= 0 else 8 + rel
                        wc0 = max(rel, 0) * 128
                        wc1 = 512 if rel >= -1 else 512 + (rel + 1) * 128
                        wsteps.append((kt, [(A.ident[:], A.masks[:, mi, :])], wc0, wc1))

                    def epi_w(par, hh=hh, h=h, qg=qg, q0=q0, ab=ab):
                        nonlocal nob
                        epilogue(hh, h, par, 2, qg, ab)
                        oi = nob % 2
                        nob += 1
                        P.op("act", lambda e: e.copy(ob[oi][:], acc[ab][hh][:]), reads=[("n_acc", ab, hh, qt) for qt in range(4)], writes=[("n_ob", oi)])
                        P.dma(lambda e: e.dma_start(
                            out=C.obtok[q0:q0 + 512, h * 64:(h + 1) * 64].rearrange("(t p) c -> p t c", p=128), in_=ob[oi][:]),
                            reads=[("n_ob", oi)], writes=[("obtok", h, qg)])
                    attn_call2(C, A, QS[qb][:, hh, :], lambda kt: kw[:, kt * 128:(kt + 1) * 128],
                               lambda kt: vwp[:, kt, :], 65, wsteps,
                               [("n_QSq", qb), ("n_QSs", qb, hh), "n_kw", "n_kwz", "n_vw", "n_vw1"], False, epi_w)
                emit_calls(C, A)

            stage_a(QORDER[0], 0, 0)
            for i in range(NG):
                if i + 1 < NG:
                    stage_a(QORDER[i + 1], (i + 1) % 2, (i + 1) % 2)
                stage_b(QORDER[i], i % 2, i % 2)


def phase_outp(C, l):
    P, nc = C.P, C.nc
    with ExitStack() as st:
        sb, ps = _alloc(C, st)
        Wa = sb("o_Wa", [128, 4, D], BF16)
        Wb = sb("o_Wb", [128, 4, D], BF16)
        Wo = sb("o_Wo", [128, 8, D], BF16)
        oa = sb("o_oa", [128, 4, 512], BF16)
        obx = sb("o_ob", [128, 4, 512], BF16)
        ga = sb("o_ga", [128, 8, 512], F32)
        gb = sb("o_gb", [128, 8, 512], F32)
        hin = sb("o_hin", [128, 4, D], F32)
        t1 = [sb("o_t1%d" % i, [128, 512], F32) for i in range(2)]
        t2 = [sb("o_t2%d" % i, [128, 512], F32) for i in range(2)]
        yT = sb("o_yT", [128, 8, 512], BF16)
        hout = [sb("o_hout%d" % i, [128, D], F32) for i in range(2)]
        oat = sb("o_oat", [128, 4, 512], BF16)
        obt = sb("o_obt", [128, 4, 512], BF16)
        identb = sb("o_identb", [128, 128], BF16)
        load_const(C, identb[:], "ident_bf", "o_identb")
        ptr = [ps("o_ptr%d" % i, [128, 4, 128], BF16) for i in range(2)]
        pa = [ps("o_pa%d" % i, [128, 512], F32) for i in range(2)]
        pb = [ps("o_pb%d" % i, [128, 512], F32) for i in range(2)]
        po = [ps("o_po%d" % i, [128, 512], F32) for i in range(2)]
        P.dma(lambda e: e.dma_start(out=Wa[:], in_=C.w["w_branch_a"][l].rearrange("(c p) f -> p c f", p=128)), writes=["o_Wa"], q="pool")
        P.dma(lambda e: e.dma_start(out=Wb[:], in_=C.w["w_branch_b"][l].rearrange("(c p) f -> p c f", p=128)), writes=["o_Wb"], q="pool")
        P.dma(lambda e: e.dma_start(out=Wo[:], in_=C.w["w_out"][l].rearrange("(c p) f -> p c f", p=128)), writes=["o_Wo"], q="pool")
        for g in range(NG):
            r0 = g * 512
            P.dma(lambda e, r0=r0: e.dma_start(out=oat[:], in_=C.oatok[r0:r0 + 512, :].rearrange("(t p) c -> p t c", p=128)),
                  reads=["oatok_all"], writes=["o_oat"])
            P.dma(lambda e, r0=r0: e.dma_start(out=obt[:], in_=C.obtok[r0:r0 + 512, :].rearrange("(t p) c -> p t c", p=128)),
                  reads=["obtok_all"], writes=["o_obt"])
            ntr = 0
            for srct, dstt, sk, dk in ((oat, oa, "o_oat", "o_oa"), (obt, obx, "o_obt", "o_ob")):
                for tt in range(4):
                    pb_ = ntr % 2
                    ntr += 1

                    def ftr(e, srct=srct, tt=tt, pb_=pb_):
                        ins = None
                        for kc in range(4):
                            ins = e.transpose(ptr[pb_][:, kc, :], srct[:, tt, kc * 128:(kc + 1) * 128], identb[:])
                        return ins
                    P.op("pe", ftr, reads=[sk, "o_identb"], writes=[("o_ptr", pb_)])
                    P.op("act", lambda e, dstt=dstt, tt=tt, pb_=pb_: e.copy(dstt[:, :, tt * 128:(tt + 1) * 128], ptr[pb_][:, :, :]),
                         reads=[("o_ptr", pb_)], writes=[(dk, tt)])
            P.dma(lambda e, r0=r0: e.dma_start(out=ga[:], in_=C.gaT[:, r0:r0 + 512].rearrange("(c p) t -> p c t", p=128)),
                  reads=["gaT_all"], writes=["o_ga"])
            P.dma(lambda e, r0=r0: e.dma_start(out=gb[:], in_=C.gbT[:, r0:r0 + 512].rearrange("(c p) t -> p c t", p=128)),
                  reads=["gbT_all"], writes=["o_gb"])
            P.dma(lambda e, r0=r0: e.dma_start(out=hin[:], in_=C.hbuf[r0:r0 + 512, :].rearrange("(t p) d -> p t d", p=128)),
                  reads=[("hdram", g)], writes=[("o_hin", tt) for tt in range(4)])
            for fb in range(8):
                b = fb % 2

                def mm(e, W, X, dstp, fb=fb):
                    ins = None
                    for kc in range(4):
                        ins = e.matmul(dstp[:], W[:, kc, fb * 128:(fb + 1) * 128], X[:, kc, :], start=(kc == 0), stop=(kc == 3))
                    return ins
                P.op("pe", lambda e, b=b, mm=mm: mm(e, Wa, oa, pa[b]), reads=["o_Wa"] + [("o_oa", tt) for tt in range(4)], writes=[("o_pa", b)])
                P.op("pe", lambda e, b=b, mm=mm: mm(e, Wb, obx, pb[b]), reads=["o_Wb"] + [("o_ob", tt) for tt in range(4)], writes=[("o_pb", b)])
                P.op("dve", lambda e, b=b, fb=fb: e.tensor_tensor(t1[b][:], pa[b][:], ga[:, fb, :], ALU.mult),
                     reads=[("o_pa", b), "o_ga"], writes=[("o_t1", b)])
                P.op("dve", lambda e, b=b, fb=fb: e.tensor_tensor(t2[b][:], pb[b][:], gb[:, fb, :], ALU.mult),
                     reads=[("o_pb", b), "o_gb"], writes=[("o_t2", b)])
                P.op("pool", lambda e, b=b, fb=fb: e.tensor_tensor(yT[:, fb, :], t1[b][:], t2[b][:], ALU.add),
                     reads=[("o_t1", b), ("o_t2", b)], writes=[("o_yT", fb)])
            yk = [("o_yT", fb) for fb in range(8)]
            for tt in range(4):
                hb = tt % 2
                for half in range(2):
                    b = half

                    def mmo(e, tt=tt, half=half, b=b):
                        ins = None
                        for fb in range(8):
                            ins = e.matmul(po[b][:], yT[:, fb, tt * 128:(tt + 1) * 128], Wo[:, fb, half * 512:(half + 1) * 512],
                                           start=(fb == 0), stop=(fb == 7))
                        return ins
                    P.op("pe", mmo, reads=yk + ["o_Wo"], writes=[("o_po", b)])
                    P.op("dve", lambda e, tt=tt, half=half, b=b, hb=hb: e.tensor_tensor(
                        hout[hb][:, half * 512:(half + 1) * 512], po[b][:], hin[:, tt, half * 512:(half + 1) * 512], ALU.add),
                        reads=[("o_po", b), ("o_hin", tt)], writes=[("o_hout", hb, half)])
                rr = r0 + tt * 128
                P.dma(lambda e, rr=rr, hb=hb: e.dma_start(out=C.hbuf[rr:rr + 128, :], in_=hout[hb][:]),
                      reads=[("o_hout", hb, 0), ("o_hout", hb, 1)], writes=[("hdram", g)])


def prep_inputs(inputs):
    bf = ml_dtypes.bfloat16
    w = {}
    for n, s in WEIGHT_SPECS:
        if n == "w_in_perm":
            continue
        w[n] = np.ascontiguousarray(np.asarray(inputs[n], dtype=np.float32).reshape(s))
    idx = []
    for c0, n in ROPE_COLS:
        for h0 in range(c0, c0 + n, 64):
            idx += list(range(h0 + 8, h0 + 16)) + list(range(h0, h0 + 8)) + list(range(h0 + 16, h0 + 64))
    w["w_in_perm"] = np.ascontiguousarray(w["w_in"][:, :, np.asarray(idx)])
    consts = make_consts()
    x = np.asarray(inputs["x"], dtype=np.float32)
    pos = np.asarray(inputs["positions"]).astype(np.int32)
    in_maps = []
    for b in range(8):
        m = {"x": np.ascontiguousarray(x[b]), "pos": np.ascontiguousarray(pos[b:b + 1])}
        m.update(w)
        m.update(consts)
        in_maps.append(m)
    return in_maps


_CACHE = {}


def kernel(**inputs):
    in_maps = prep_inputs(inputs)
    if "nc" not in _CACHE:
        _CACHE["nc"] = build("all")[0]
    nc = _CACHE["nc"]
    res = run_bass_kernel_spmd(nc, in_maps, core_ids=list(range(8)))
    out = np.stack([np.asarray(r["out"], dtype=np.float32).reshape(T, D) for r in res.results], 0)
    return out
```
